# Optimizing a Trainium2 kernel written in Bass

```python
import numpy as np
import jax
import jax.numpy as jnp
from jax import lax

D_MODEL = 1024
BATCH = 2
SEQ = 16384
DEPTH = 4

HEAD_DIM = 64
RET_HEADS = 8
RWKV_HEADS = 8
RET_WIDTH = RET_HEADS * HEAD_DIM
RWKV_WIDTH = RWKV_HEADS * HEAD_DIM
MIX_WIDTH = RET_WIDTH + RWKV_WIDTH
RET_CHUNK = 128
ROPE_BASE = 10000.0
RET_GN_EPS = 1e-6
W_LORA = 64
A_LORA = 64
G_LORA = 128
RWKV_GN_EPS = 64e-5
RWKV_SHIFT_WIDTH = 3 * RWKV_WIDTH + W_LORA + A_LORA + G_LORA
EVEN_IN = 4 * RET_WIDTH + RWKV_SHIFT_WIDTH
SWA_HEADS = 16
SWA_KV_HEADS = 4
SWA_GROUP = SWA_HEADS // SWA_KV_HEADS
SWA_WINDOW = 128
SWA_WIDTH = SWA_HEADS * HEAD_DIM
SWA_QKV = (SWA_HEADS + 2 * SWA_KV_HEADS) * HEAD_DIM
D_FF = -(-8 * D_MODEL // (3 * 256)) * 256
RMS_EPS = 1e-6
N_EVEN = (DEPTH + 1) // 2
N_ODD = DEPTH // 2

kernel_name = 'hybrid_retention_rwkv7_swa_sink_trunk'


def rms_norm(x, g):
    xf = x.astype(jnp.float32)
    y = xf * lax.rsqrt(jnp.mean(xf * xf, axis=-1, keepdims=True) + RMS_EPS)
    return (y * g.astype(jnp.float32)).astype(x.dtype)


def head_group_norm(y, eps, gain=None, bias=None):
    yf = y.astype(jnp.float32)
    mu = jnp.mean(yf, axis=-1, keepdims=True)
    var = jnp.mean(jnp.square(yf - mu), axis=-1, keepdims=True)
    out = (yf - mu) * lax.rsqrt(var + eps)
    if gain is not None:
        out = out * gain.astype(jnp.float32) + bias.astype(jnp.float32)
    return out.astype(y.dtype)


def rotary(x, pos):
    half = x.shape[-1] // 2
    inv_freq = ROPE_BASE ** (-jnp.linspace(0.0, 1.0, half))
    ang = pos[:, None] * inv_freq[None, :]
    cos = jnp.cos(ang)[None, :, None, :].astype(x.dtype)
    sin = jnp.sin(ang)[None, :, None, :].astype(x.dtype)
    x1, x2 = x[..., :half], x[..., half:]
    return jnp.concatenate([x1 * cos - x2 * sin, x1 * sin + x2 * cos], axis=-1)


def token_shift(z):
    return jnp.pad(z, ((0, 0), (1, 0), (0, 0)))[:, :-1]


def chunk_retention(q, k, v):
    B, T, H, D = q.shape
    C = RET_CHUNK
    N = T // C
    log_gamma = jnp.log1p(-(2.0 ** (-5.0 - jnp.arange(H, dtype=jnp.float32))))
    q = q.reshape(B, N, C, H, D)
    k = k.reshape(B, N, C, H, D)
    v = v.reshape(B, N, C, H, D)
    idx = jnp.arange(C, dtype=jnp.float32)
    rel = idx[:, None] - idx[None, :]
    inner_decay = jnp.where(rel >= 0, jnp.exp(jnp.maximum(rel, 0.0)[None] * log_gamma[:, None, None]), 0.0)
    scores = jnp.einsum('bnihd,bnjhd->bnhij', q, k) * inner_decay.astype(q.dtype)
    o_inner = jnp.einsum('bnhij,bnjhd->bnihd', scores, v)
    k_dec = jnp.exp((C - 1 - idx)[:, None] * log_gamma[None, :]).astype(q.dtype)
    kv = jnp.einsum('bnjhd,bnjhe->nbhde', k * k_dec[:, :, None], v)
    chunk_decay = jnp.exp(C * log_gamma)[None, :, None, None].astype(kv.dtype)

    def step(state, kv_c):
        return chunk_decay * state + kv_c, state

    _, s_prev = lax.scan(step, jnp.zeros_like(kv[0]), kv)
    q_dec = jnp.exp((idx + 1.0)[:, None] * log_gamma[None, :]).astype(q.dtype)
    o_cross = jnp.einsum('bnihd,nbhde->bnihe', q * q_dec[:, :, None], s_prev)
    return (o_inner + o_cross).reshape(B, T, H, D)


def rwkv7_scan(r, w, k, v, a, b):
    B, T, H, D = r.shape

    def step(state, inp):
        r_t, w_t, k_t, v_t, a_t, b_t = inp
        sa = jnp.einsum('bhij,bhj->bhi', state, a_t)
        state = state * w_t[:, :, None, :] + sa[..., None] * b_t[:, :, None, :] + v_t[..., None] * k_t[:, :, None, :]
        return state, jnp.einsum('bhij,bhj->bhi', state, r_t)

    xs = tuple(jnp.moveaxis(t, 1, 0) for t in (r, w, k, v, a, b))
    _, y = lax.scan(step, jnp.zeros((B, H, D, D), r.dtype), xs)
    return jnp.moveaxis(y, 0, 1)


def retention_rwkv_mixer(h, w_in, w_out, mu, w0, w_up, a0, a_up, g_up, k_k, k_a, r_k, ln_g, ln_b):
    B, T, _ = h.shape

    def heads(t):
        return t.reshape(B, T, -1, HEAD_DIM)

    z = h @ w_in
    z_ret, z_rwkv = z[..., :4 * RET_WIDTH], z[..., 4 * RET_WIDTH:]

    q, k, v, g = jnp.split(z_ret, 4, axis=-1)
    pos = jnp.arange(T, dtype=jnp.float32)
    q = rotary(heads(q), pos)
    k = rotary(heads(k), pos) * HEAD_DIM ** -0.5
    o = chunk_retention(q, k, heads(v))
    ret_out = head_group_norm(o, RET_GN_EPS).reshape(B, T, RET_WIDTH) * jax.nn.silu(g)

    zs = z_rwkv + mu * (token_shift(z_rwkv) - z_rwkv)
    split_at = [RWKV_WIDTH, 2 * RWKV_WIDTH, 3 * RWKV_WIDTH,
                3 * RWKV_WIDTH + W_LORA, 3 * RWKV_WIDTH + W_LORA + A_LORA]
    rr, kr, vr, wl, al, gl = jnp.split(zs, split_at, axis=-1)
    w_log = -jax.nn.softplus(-(w0 + jnp.tanh(wl) @ w_up)) - 0.5
    decay = jnp.exp(-jnp.exp(w_log))
    a = jax.nn.sigmoid(a0 + al @ a_up)
    gate = jax.nn.sigmoid(gl) @ g_up
    kkf = heads(kr * k_k).astype(jnp.float32)
    kk = (kkf / jnp.maximum(jnp.linalg.norm(kkf, axis=-1, keepdims=True), 1e-12)).astype(kr.dtype)
    kr = kr * (1.0 + (a - 1.0) * k_a)
    r_h, k_h, v_h = heads(rr), heads(kr), heads(vr)
    y = rwkv7_scan(r_h, heads(decay), k_h, v_h, -kk, kk * heads(a))
    y = head_group_norm(y, RWKV_GN_EPS, ln_g, ln_b)
    bonus = jnp.sum(r_h * k_h * r_k, axis=-1, keepdims=True) * v_h
    rwkv_out = (y + bonus).reshape(B, T, RWKV_WIDTH) * gate

    return jnp.concatenate([ret_out, rwkv_out], axis=-1) @ w_out


def swa_sink_mixer(h, w_qkv, b_qkv, sinks, w_o, b_o):
    B, T, _ = h.shape
    W = SWA_WINDOW
    NB = T // W
    z = h @ w_qkv + b_qkv
    q, k, v = jnp.split(z, [SWA_WIDTH, SWA_WIDTH + SWA_KV_HEADS * HEAD_DIM], axis=-1)
    q = q.reshape(B, NB, W, SWA_KV_HEADS, SWA_GROUP, HEAD_DIM)
    k = k.reshape(B, NB, W, SWA_KV_HEADS, HEAD_DIM)
    v = v.reshape(B, NB, W, SWA_KV_HEADS, HEAD_DIM)

    def with_prev(t):
        prev = jnp.pad(t, ((0, 0), (1, 0), (0, 0), (0, 0), (0, 0)))[:, :-1]
        return jnp.concatenate([prev, t], axis=2)

    kb, vb = with_prev(k), with_prev(v)
    s = jnp.einsum('bnqhgd,bnkhd->bnhgqk', q, kb).astype(jnp.float32) * HEAD_DIM ** -0.5
    qi = jnp.arange(W)[:, None]
    kj = jnp.arange(2 * W)[None, :]
    rel = qi + W - kj
    band = (rel >= 0) & (rel < W)
    valid = band[None] & ((jnp.arange(NB)[:, None, None] > 0) | (kj[None] >= W))
    s = jnp.where(valid[None, :, None, None], s, -jnp.inf)
    sink = sinks.astype(jnp.float32).reshape(SWA_KV_HEADS, SWA_GROUP)[None, None, :, :, None, None]
    m = jnp.maximum(jnp.max(s, axis=-1, keepdims=True), sink)
    e = jnp.exp(s - m)
    p = e / (jnp.sum(e, axis=-1, keepdims=True) + jnp.exp(sink - m))
    o = jnp.einsum('bnhgqk,bnkhd->bnqhgd', p.astype(vb.dtype), vb)
    return o.reshape(B, T, SWA_WIDTH) @ w_o + b_o


def swiglu(h, w_gate, w_up, w_down):
    return (jax.nn.silu(h @ w_gate) * (h @ w_up)) @ w_down


def setup_inputs(seed: int = 0) -> dict:
    key = jax.random.key(seed)
    ks = jax.random.split(key, 25)
    f32 = jnp.float32
    D, H = D_MODEL, RWKV_HEADS

    def nrm(k, shape, scale):
        return jax.random.normal(k, shape, f32) * scale

    return {
        'x': nrm(ks[0], (BATCH, SEQ, D), 1.0),
        'norm1_g': 1.0 + nrm(ks[1], (DEPTH, D), 0.02),
        'norm2_g': 1.0 + nrm(ks[2], (DEPTH, D), 0.02),
        'final_g': 1.0 + nrm(ks[3], (D,), 0.02),
        'even_w_in': nrm(ks[4], (N_EVEN, D, EVEN_IN), D ** -0.5),
        'even_w_out': nrm(ks[5], (N_EVEN, MIX_WIDTH, D), MIX_WIDTH ** -0.5),
        'rwkv_mu': jax.random.uniform(ks[6], (N_EVEN, RWKV_SHIFT_WIDTH), f32),
        'rwkv_w0': nrm(ks[7], (N_EVEN, RWKV_WIDTH), 0.5),
        'rwkv_w_up': nrm(ks[8], (N_EVEN, W_LORA, RWKV_WIDTH), 0.5 * W_LORA ** -0.5),
        'rwkv_a0': nrm(ks[9], (N_EVEN, RWKV_WIDTH), 0.5),
        'rwkv_a_up': nrm(ks[10], (N_EVEN, A_LORA, RWKV_WIDTH), 0.5 * A_LORA ** -0.5),
        'rwkv_g_up': nrm(ks[11], (N_EVEN, G_LORA, RWKV_WIDTH), G_LORA ** -0.5),
        'rwkv_k_k': 0.85 + nrm(ks[12], (N_EVEN, RWKV_WIDTH), 0.05),
        'rwkv_k_a': 1.0 + nrm(ks[13], (N_EVEN, RWKV_WIDTH), 0.05),
        'rwkv_r_k': nrm(ks[14], (N_EVEN, H, HEAD_DIM), 0.1),
        'rwkv_ln_g': 1.0 + nrm(ks[15], (N_EVEN, H, HEAD_DIM), 0.02),
        'rwkv_ln_b': nrm(ks[16], (N_EVEN, H, HEAD_DIM), 0.02),
        'swa_w_qkv': nrm(ks[17], (N_ODD, D, SWA_QKV), D ** -0.5),
        'swa_b_qkv': nrm(ks[18], (N_ODD, SWA_QKV), 0.02),
        'swa_sinks': nrm(ks[19], (N_ODD, SWA_HEADS), 0.5),
        'swa_w_o': nrm(ks[20], (N_ODD, SWA_WIDTH, D), SWA_WIDTH ** -0.5),
        'swa_b_o': nrm(ks[21], (N_ODD, D), 0.02),
        'ffn_w_gate': nrm(ks[22], (DEPTH, D, D_FF), D ** -0.5),
        'ffn_w_up': nrm(ks[23], (DEPTH, D, D_FF), D ** -0.5),
        'ffn_w_down': nrm(ks[24], (DEPTH, D_FF, D), D_FF ** -0.5),
    }


def reference(x, norm1_g, norm2_g, final_g, even_w_in, even_w_out, rwkv_mu, rwkv_w0, rwkv_w_up,
              rwkv_a0, rwkv_a_up, rwkv_g_up, rwkv_k_k, rwkv_k_a, rwkv_r_k, rwkv_ln_g, rwkv_ln_b,
              swa_w_qkv, swa_b_qkv, swa_sinks, swa_w_o, swa_b_o, ffn_w_gate, ffn_w_up, ffn_w_down):
    h = x
    for layer in range(DEPTH):
        i = layer // 2
        n = rms_norm(h, norm1_g[layer])
        if layer % 2 == 0:
            mix = retention_rwkv_mixer(n, even_w_in[i], even_w_out[i], rwkv_mu[i], rwkv_w0[i], rwkv_w_up[i],
                                       rwkv_a0[i], rwkv_a_up[i], rwkv_g_up[i], rwkv_k_k[i], rwkv_k_a[i],
                                       rwkv_r_k[i], rwkv_ln_g[i], rwkv_ln_b[i])
        else:
            mix = swa_sink_mixer(n, swa_w_qkv[i], swa_b_qkv[i], swa_sinks[i], swa_w_o[i], swa_b_o[i])
        h = h + mix
        n = rms_norm(h, norm2_g[layer])
        h = h + swiglu(n, ffn_w_gate[layer], ffn_w_up[layer], ffn_w_down[layer])
    return rms_norm(h, final_g)
```

```python
import math
from contextlib import ExitStack

import numpy as np
import concourse.bass as bass
import concourse.mybir as mybir
from concourse.bass_utils import run_bass_kernel_spmd

F32 = mybir.dt.float32
BF16 = mybir.dt.bfloat16
AF = mybir.ActivationFunctionType
ALU = mybir.AluOpType
AX = mybir.AxisListType

D = 1024
HD = 64
DFF = 2816
EVEN_IN = 3840
SWA_QKV = 1536
RMS_EPS = 1e-6
RET_GN_EPS = 1e-6
RWKV_GN_EPS = 64e-5
NEG = -30000.0

ENGS = ("pe", "act", "dve", "pool", "sp")


class Op:
    __slots__ = ("eng", "fn", "deps", "signal", "seq", "is_dma", "dkey", "dval", "epoch")

    def __init__(self, eng, fn, deps, is_dma=False):
        self.eng = eng
        self.fn = fn
        self.deps = deps
        self.signal = False
        self.seq = 0
        self.is_dma = is_dma
        self.dkey = None
        self.dval = 0
        self.epoch = 0


class Res:
    __slots__ = ("w", "r", "rd")

    def __init__(self):
        self.w = None
        self.r = {}
        self.rd = []


class Sched:
    def __init__(self, nc):
        self.nc = nc
        self.segs = []
        self.epoch = 0
        self.new_segment()

    def new_segment(self, loop=None):
        if self.segs and loop is None and self.segs[-1]["loop"] is None and \
                not any(self.segs[-1]["ops"][e] for e in ENGS):
            return
        self.cur = {"loop": loop, "ops": {e: [] for e in ENGS}, "dk": {}, "dkeng": {}}
        self.segs.append(self.cur)
        self.epoch += 1

    def _live(self, o):
        return o is not None and o.epoch == self.epoch

    def _deps(self, eng, is_dma, reads, writes):
        deps = []
        for r in reads:
            if self._live(r.w):
                deps.append(r.w)
        strict = is_dma or eng != "pe"
        for w in writes:
            x = w.w
            if self._live(x) and (x.is_dma or strict or x.eng != eng):
                deps.append(x)
            for e, x in w.r.items():
                if self._live(x) and (strict or e != eng):
                    deps.append(x)
            for x in w.rd:
                if self._live(x):
                    deps.append(x)
        return deps

    def _commit(self, op, reads, writes):
        op.epoch = self.epoch
        for r in reads:
            if op.is_dma:
                r.rd = [x for x in r.rd if x.epoch == self.epoch]
                r.rd.append(op)
            else:
                r.r[op.eng] = op
        for w in writes:
            w.w = op
            w.r = {}
            w.rd = []
        self.cur["ops"][op.eng].append(op)

    def op(self, eng, fn, reads=(), writes=()):
        o = Op(eng, fn, self._deps(eng, False, reads, writes))
        self._commit(o, reads, writes)
        return o

    def dma(self, eng, fn, key, reads=(), writes=()):
        o = Op(eng, fn, self._deps(eng, True, reads, writes), is_dma=True)
        dk = self.cur["dk"]
        dk[key] = dk.get(key, 0) + 16
        self.cur["dkeng"][key] = eng
        o.dkey = key
        o.dval = dk[key]
        self._commit(o, reads, writes)
        return o

    def emit(self, stack):
        nc = self.nc
        segs = self.segs
        nsl = []
        for seg in segs:
            ns = {}
            for e in ENGS:
                ops = seg["ops"][e]
                for i, o in enumerate(ops):
                    o.seq = i
                last = None
                for o in ops:
                    if not o.is_dma:
                        last = o
                if last is not None:
                    last.signal = True
            for e in ENGS:
                covered = {}
                for o in seg["ops"][e]:
                    best = {}
                    keep = []
                    for d in o.deps:
                        if d.is_dma:
                            keep.append(d)
                            continue
                        if d.eng == e and e == "pe":
                            continue
                        b = best.get(d.eng)
                        if b is None or d.seq > b.seq:
                            best[d.eng] = d
                    for eng2, d in best.items():
                        if covered.get(eng2, -1) >= d.seq:
                            continue
                        covered[eng2] = d.seq
                        d.signal = True
                        keep.append(d)
                    o.deps = keep
            for e in ENGS:
                c = 0
                for o in seg["ops"][e]:
                    if not o.is_dma and o.signal:
                        c += 1
                        o.seq = c
                ns[e] = c + 1
            nsl.append(ns)
        base = []
        dbase = []
        cnt = {e: 0 for e in ENGS}
        dcnt = {}
        for si, seg in enumerate(segs):
            base.append(dict(cnt))
            dbase.append(dict(dcnt))
            n = 1 if seg["loop"] is None else (seg["loop"][1] - seg["loop"][0])
            for e in ENGS:
                cnt[e] += n * nsl[si][e]
            for k, v in seg["dk"].items():
                dcnt[k] = dcnt.get(k, 0) + n * v
        esem = {e: stack.enter_context(nc.semaphore("s_" + e)) for e in ENGS}
        dsem = {}
        for k in dcnt:
            dsem[k] = stack.enter_context(nc.semaphore("d%d" % len(dsem)))
        block = stack.enter_context(nc.Block())

        def run(ename, eng):
            regpool = []

            def getreg(j):
                while len(regpool) <= j:
                    regpool.append(eng.alloc_register("%s_r%d" % (ename, len(regpool))))
                return regpool[j]

            for si, seg in enumerate(segs):
                ops = seg["ops"][ename]
                looped = seg["loop"] is not None
                bases = {}
                for X in ENGS:
                    bases[("e", X)] = (esem[X], base[si][X], nsl[si][X])
                for k in seg["dk"]:
                    bases[("d", k)] = (dsem[k], dbase[si].get(k, 0), seg["dk"][k])
                used = []

                def plan(body_fn):
                    body_fn(True)

                bregs = {}

                def W(kk, local, dry):
                    sem, b, per = bases[kk]
                    if dry:
                        if kk not in used:
                            used.append(kk)
                        return
                    if not looped:
                        eng.wait_ge(sem, b + local)
                    else:
                        rs = getreg(0)
                        eng.reg_add(rs, bregs[kk], local)
                        eng.wait_ge(sem, rs)

                def body(dry):
                    for X in ENGS:
                        if X == ename:
                            continue
                        if not looped and base[si][X] == 0:
                            continue
                        W(("e", X), 0, dry)
                    waited = {}
                    for o in ops:
                        for d in o.deps:
                            if d.is_dma:
                                key = ("d", d.dkey)
                                v = d.dval
                            else:
                                key = ("e", d.eng)
                                v = d.seq
                            if waited.get(key, 0) >= v:
                                continue
                            waited[key] = v
                            W(key, v, dry)
                        if dry:
                            continue
                        ins = o.fn(eng, self._it)
                        if o.is_dma:
                            ins.then_inc(dsem[o.dkey], 16)
                        elif o.signal:
                            ins.then_inc(esem[ename], 1)
                    if nsl[si][ename] > 1:
                        W(("e", ename), nsl[si][ename] - 1, dry)
                    for k, e2 in seg["dkeng"].items():
                        if e2 == ename:
                            W(("d", k), seg["dk"][k], dry)
                    if not dry:
                        eng.sem_inc(esem[ename], 1)

                if not looped:
                    self._it = None
                    body(False)
                else:
                    body(True)
                    for j, kk in enumerate(used):
                        bregs[kk] = getreg(j + 1)
                        eng.reg_mov(bregs[kk], bases[kk][1])
                    with eng.Fori(seg["loop"][0], seg["loop"][1]) as it:
                        self._it = it
                        body(False)
                        for kk in used:
                            eng.reg_add(bregs[kk], bregs[kk], bases[kk][2])

        @block.tensor
        def _(t):
            run("pe", t)

        @block.scalar
        def _(t):
            run("act", t)

        @block.vector
        def _(t):
            run("dve", t)

        @block.gpsimd
        def _(t):
            run("pool", t)

        @block.sync
        def _(t):
            run("sp", t)


class V:
    __slots__ = ("ap", "res")

    def __init__(self, ap, res=None):
        self.ap = ap
        self.res = res if res is not None else Res()

    def __getitem__(self, key):
        return V(self.ap[key], self.res)

    def re(self, pat, **kw):
        return V(self.ap.rearrange(pat, **kw), self.res)

    def bc(self, shape):
        return V(self.ap.to_broadcast(list(shape)), self.res)

    def un(self, axis):
        return V(self.ap.unsqueeze(axis), self.res)

    def sub(self, key):
        return V(self.ap[key], Res())

    def bitcast(self, dt):
        return V(self.ap.bitcast(dt), self.res)


def _res(*vs):
    out = []
    for v in vs:
        if isinstance(v, V):
            out.append(v.res)
    return out


def _ap(v):
    return v.ap if isinstance(v, V) else v


class Ctx:
    def __init__(self, nc, stack, arena_words):
        self.nc = nc
        self.S = Sched(nc)
        self.arena = stack.enter_context(nc.sbuf_tensor("arena", [128, arena_words], F32))
        self.arena_words = arena_words
        self.off = 0
        self.psum = stack.enter_context(nc.psum_tensor("psum", [128, 8, 512], F32))
        self.banks = [V(self.psum[:, b, :]) for b in range(8)]
        self.prr = 0
        self.nrot = 7
        self.consts = {}

    def sb(self, shape, dt=F32):
        n = 1
        for s in shape[1:]:
            n *= s
        n4 = n if dt == F32 else (n + 1) // 2
        n4 = (n4 + 7) // 8 * 8
        if self.off + n4 > self.arena_words:
            raise RuntimeError("arena overflow: need %d words" % (self.off + n4))
        a = self.arena[0:shape[0], self.off:self.off + n4]
        self.off += n4
        if dt != F32:
            a = a.bitcast(dt)
        a = a[:, 0:n]
        if len(shape) == 3:
            a = a.rearrange("p (a b) -> p a b", b=shape[2])
        elif len(shape) == 4:
            a = a.rearrange("p (a b c) -> p a b c", b=shape[2], c=shape[3])
        return V(a)

    def bank(self):
        b = self.banks[self.prr]
        self.prr = (self.prr + 1) % self.nrot
        return b

    def mm(self, out, lhsT, rhs, start=True, stop=True, extra_r=()):
        o, l, r = out.ap, lhsT.ap, rhs.ap
        self.S.op("pe", lambda e, it: e.matmul(o, l, r, start=start, stop=stop),
                  reads=_res(lhsT, rhs) + list(extra_r), writes=_res(out))

    def tr(self, out, in_, ident):
        o, i, d = out.ap, in_.ap, ident.ap
        self.S.op("pe", lambda e, it: e.transpose(o, i, d), reads=_res(in_, ident), writes=_res(out))

    def act(self, out, in_, func, bias=None, scale=None, accum=None, extra_w=()):
        o, i = out.ap, in_.ap
        kw = {}
        if bias is not None:
            kw["bias"] = _ap(bias)
        if scale is not None:
            kw["scale"] = _ap(scale)
        if accum is not None:
            kw["accum_out"] = accum.ap
        self.S.op("act", lambda e, it: e.activation(out=o, in_=i, func=func, **kw),
                  reads=_res(in_, bias, scale), writes=_res(out, accum) + list(extra_w))

    def tt(self, eng, out, in0, in1, op):
        o, a, b = out.ap, in0.ap, in1.ap
        self.S.op(eng, lambda e, it: e.tensor_tensor(out=o, in0=a, in1=b, op=op),
                  reads=_res(in0, in1), writes=_res(out))

    def ts(self, eng, out, in0, s1, s2, op0, op1=None):
        o, a = out.ap, in0.ap
        x1, x2 = _ap(s1), _ap(s2)
        if op1 is None:
            fn = lambda e, it: e.tensor_scalar(out=o, in0=a, scalar1=x1, scalar2=None, op0=op0)
        else:
            fn = lambda e, it: e.tensor_scalar(out=o, in0=a, scalar1=x1, scalar2=x2, op0=op0, op1=op1)
        self.S.op(eng, fn, reads=_res(in0, s1, s2), writes=_res(out))

    def stt(self, out, in0, scalar, in1, op0, op1):
        o, a, b, s = out.ap, in0.ap, in1.ap, _ap(scalar)
        self.S.op("dve", lambda e, it: e.scalar_tensor_tensor(out=o, in0=a, scalar=s, in1=b, op0=op0, op1=op1),
                  reads=_res(in0, scalar, in1), writes=_res(out))

    def copy(self, eng, out, in_):
        o, i = out.ap, in_.ap
        if eng == "act":
            self.S.op("act", lambda e, it: e.activation(out=o, in_=i, func=AF.Copy), reads=_res(in_), writes=_res(out))
        else:
            self.S.op(eng, lambda e, it: e.tensor_copy(out=o, in_=i), reads=_res(in_), writes=_res(out))

    def red(self, out, in_, op):
        o, i = out.ap, in_.ap
        self.S.op("dve", lambda e, it: e.tensor_reduce(out=o, in_=i, axis=AX.X, op=op), reads=_res(in_), writes=_res(out))

    def scan(self, out, d0, d1, op0, op1, initial=0.0):
        o, a, b = out.ap, d0.ap, d1.ap
        self.S.op("dve", lambda e, it: e.tensor_tensor_scan(out=o, data0=a, data1=b, initial=initial, op0=op0, op1=op1),
                  reads=_res(d0, d1), writes=_res(out))

    def recip(self, out, in_):
        o, i = out.ap, in_.ap
        self.S.op("dve", lambda e, it: e.reciprocal(out=o, in_=i), reads=_res(in_), writes=_res(out))

    def memset(self, eng, out, val):
        o = out.ap
        self.S.op(eng, lambda e, it: e.memset(o, val), writes=_res(out))

    def dma(self, eng, out, in_, key):
        o, i = out.ap, in_.ap

        def fn(e, it):
            oo = o(it) if callable(o) else o
            ii = i(it) if callable(i) else i
            return e.dma_start(out=oo, in_=ii)
        self.S.dma(eng, fn, key, reads=_res(in_), writes=_res(out))

    def rstd(self, out, ssum, scale, eps, nhalf, tmp):
        self.act(tmp, ssum, AF.Ln, bias=self.const_col(eps), scale=scale)
        self.act(out, tmp, AF.Exp, scale=-0.5)

    def const_col(self, val):
        if val not in self.consts:
            t = self.sb([128, 1])
            self.memset("dve", t, float(val))
            self.consts[val] = t
        return self.consts[val]


CT = {}


def _ctab_layout():
    off = 0
    for name, w in (("ident", 128), ("bones", 128), ("swa_m", 768), ("ret_DT", 1024), ("ret_qdec", 512),
                    ("ret_kdec", 8), ("ret_gC", 4), ("rw_mNM", 256), ("rw_mL", 128), ("nhalf", 128)):
        CT[name] = (off, w)
        off += w
    return off


CT_W = _ctab_layout()


def make_ctab():
    t = np.zeros((128, CT_W), np.float64)

    def put(name, arr):
        o, w = CT[name]
        t[:, o:o + w] = np.asarray(arr, np.float64).reshape(128, w)

    idx = np.arange(128)
    put("ident", np.eye(128))
    bo = np.zeros((128, 128))
    bo[:64, :64] = 1
    bo[64:, 64:] = 1
    put("bones", bo)
    q = idx[:, None]
    j = idx[None, :]
    cur = np.where(j <= q, 0.0, NEG)
    prv = np.where(j > q, 0.0, NEG)
    allm = np.full((128, 128), NEG)
    put("swa_m", np.concatenate([cur, prv, prv, cur, cur, allm], axis=1))
    lg = np.log1p(-(2.0 ** (-5.0 - np.arange(8, dtype=np.float64))))
    rel = idx[None, :] - idx[:, None]
    DT = np.zeros((128, 8, 128))
    for h in range(8):
        DT[:, h, :] = np.where(rel >= 0, np.exp(np.maximum(rel, 0) * lg[h]), 0.0)
    put("ret_DT", DT)
    qd = np.zeros((128, 4, 128))
    gC = np.zeros((128, 4))
    for h in range(8):
        rows = slice((h % 2) * 64, (h % 2) * 64 + 64)
        qd[rows, h // 2, :] = np.exp((idx + 1.0) * lg[h])[None, :]
        gC[rows, h // 2] = np.exp(128 * lg[h])
    put("ret_qdec", qd)
    put("ret_gC", gC)
    put("ret_kdec", np.exp((127 - idx)[:, None] * lg[None, :]))
    r = idx[:, None]
    s = idx[None, :]
    put("rw_mNM", np.concatenate([(r < s) * 1.0, (r <= s) * 1.0], axis=1))
    put("rw_mL", (r > s) * 1.0)
    put("nhalf", np.full((128, 128), -0.5))
    return t.astype(np.float32)


def make_rope(T):
    half = HD // 2
    inv_freq = (10000.0 ** (-np.linspace(0.0, 1.0, half))).astype(np.float32)
    pos = np.arange(T, dtype=np.float32)
    ang = (pos[:, None] * inv_freq[None, :]).astype(np.float32).astype(np.float64)
    c = np.cos(ang)
    s = np.sin(ang)
    cc = np.concatenate([c, c], axis=1)
    ss = np.concatenate([-s, s], axis=1)
    return np.concatenate([cc, ss, cc * 0.125, ss * 0.125], axis=1).astype(np.float32)


class DT_:
    def __init__(self, ap2d):
        self.ap = ap2d
        self.ap4 = ap2d.rearrange("(n u p) d -> n u p d", u=2, p=128)

    def __call__(self, idx):
        return V(self.ap[idx * 128:(idx + 1) * 128, :])

    def pair(self, n0):
        ap4 = self.ap4
        if n0 == 0:
            return V(lambda it: ap4[it].rearrange("u p d -> p u d"))
        return V(lambda it: ap4[it + n0].rearrange("u p d -> p u d"))


class Prog:
    def __init__(self, T, plan):
        self.T = T
        self.plan = plan
        self.NT = T // 128

    def build(self):
        T = self.T
        nc = bass.Bass("TRN2", target_bir_lowering=False)
        self.nc = nc

        self.dins = {}
        self.x = self.din("x")
        self.ctab_d = self.din("ctab")
        self.rope_d = self.din("rope")
        self.y = nc.dram_tensor("y", [T, D], F32, kind="ExternalOutput").ap()
        self.hbuf = nc.dram_tensor("hbuf", [T, D], F32, kind="Internal").ap()

        with ExitStack() as st:
            c = Ctx(nc, st, 53000)
            self.c = c
            NT = self.NT
            self.x_t = DT_(self.x)
            self.h_t = DT_(self.hbuf)
            self.y_t = DT_(self.y)
            self.identf = self.ct_load("ident", "ct_ident")
            self.bones = self.ct_load("bones", "ct_bones")
            self.nhalf = self.ct_load("nhalf", "ct_nhalf")
            self.identb = c.sb([128, 128], BF16)
            c.copy("dve", self.identb, self.identf)
            for v_ in (RMS_EPS, RET_GN_EPS, RWKV_GN_EPS):
                c.const_col(v_)
            base = c.off
            src = self.x_t
            for kind, layer, last in self.plan:
                c.off = base
                c.S.new_segment()
                dst = self.y_t if last else self.h_t
                if kind == "B":
                    self.pass_ffn(layer, src, dst, last)
                elif kind == "O":
                    self.pass_swa(layer // 2, layer, src, dst)
                elif kind == "E":
                    self.pass_even(layer // 2, layer, src, dst)
                src = self.h_t
            c.S.emit(st)
        return nc

    SHAPES = {
        "norm1_g": [4, D], "norm2_g": [4, D], "final_g": [D], "even_w_in": [2, D, EVEN_IN], "even_w_out": [2, D, D],
        "rwkv_mu": [2, 1792], "rwkv_w0": [2, 512], "rwkv_w_up": [2, 64, 512], "rwkv_a0": [2, 512],
        "rwkv_a_up": [2, 64, 512], "rwkv_g_up": [2, 128, 512], "rwkv_k_k": [2, 512], "rwkv_k_a": [2, 512],
        "rwkv_r_k": [2, 512], "rwkv_ln_g": [2, 512], "rwkv_ln_b": [2, 512], "swa_w_qkv": [2, D, SWA_QKV],
        "swa_b_qkv": [2, SWA_QKV], "swa_sinks": [2, 16], "swa_w_o": [2, D, D], "swa_b_o": [2, D],
        "ffn_w_gate": [4, D, DFF], "ffn_w_up": [4, D, DFF], "ffn_w_down": [4, DFF, D], "ctab": [128, CT_W],
    }

    def din(self, name):
        if name not in self.dins:
            if name == "x":
                shape = [self.T, D]
            elif name == "rope":
                shape = [self.T, 256]
            else:
                shape = self.SHAPES[name]
            self.dins[name] = self.nc.dram_tensor(name, list(shape), F32, kind="ExternalInput").ap()
        return self.dins[name]

    @staticmethod
    def dtile(ap2d, idx):
        if isinstance(idx, int):
            return V(ap2d[idx * 128:(idx + 1) * 128, :])
        m, a = idx
        assert m == 2
        ap4 = ap2d.rearrange("(n u p) d -> n u p d", u=2, p=128)
        n0, u = a // 2, a % 2
        if n0 == 0:
            return V(lambda it: ap4[it, u, :, :])
        return V(lambda it: ap4[it + n0, u, :, :])

    def ct_load(self, name, key):
        o, w = CT[name]
        t = self.c.sb([128, w])
        self.c.dma("sp", t, V(self.ctab_d[:, o:o + w]), key)
        return t

    def load_w(self, dst, src_ap, key, kt, eng="pool"):
        c = self.c
        for k in range(kt):
            c.dma(eng, dst[:, k, :], V(src_ap[k * 128:(k + 1) * 128, :], Res()), key)

    def bcast_load(self, vec_ap, n, key):
        c = self.c
        t = c.sb([128, n])
        c.dma("sp", t, V(vec_ap.partition_broadcast(128)), key)
        return t

    def col_load(self, vec_ap, nt, key):
        c = self.c
        t = c.sb([128, nt])
        c.S.dma("sp", (lambda o, i: (lambda e, it: e.dma_start(out=o, in_=i, allow_slow_non_contiguous=True)))(
            t.ap, vec_ap.rearrange("(t p) -> p t", p=128)), key, writes=[t.res])
        return t

    def rmsnorm_to_bf16(self, hb, gb, nb, junk, ss, tmp, rs):
        c = self.c
        c.act(junk, hb, AF.Square, accum=ss)
        c.rstd(rs, ss, 1.0 / D, RMS_EPS, self.nhalf[:, 0:1], tmp)
        c.stt(nb, hb, rs, gb, ALU.mult, ALU.mult)

    def transpose_bf16(self, dst3, src, nblk, evac="act"):
        c = self.c
        for g0 in range(0, nblk, 8):
            n = min(8, nblk - g0)
            bk = c.bank().bitcast(BF16)
            for j in range(n):
                c.tr(bk[:, j * 128:(j + 1) * 128], src[:, (g0 + j) * 128:(g0 + j + 1) * 128], self.identb)
            c.copy(evac, dst3[:, g0:g0 + n, :], bk[:, 0:n * 128].re("p (a b) -> p a b", b=128))

    def pass_ffn(self, layer, src, dst, last):
        c = self.c
        TB = 2
        ntile = self.NT // TB
        wg = c.sb([128, 8, DFF], BF16)
        wu = c.sb([128, 8, DFF], BF16)
        wd = c.sb([128, 22, D], BF16)
        self.load_w(wg, self.din("ffn_w_gate")[layer], "wg", 8)
        self.load_w(wu, self.din("ffn_w_up")[layer], "wu", 8)
        self.load_w(wd, self.din("ffn_w_down")[layer], "wd", 22)
        g2 = self.bcast_load(self.din("norm2_g")[layer], D, "g2")
        gf = self.bcast_load(self.din("final_g"), D, "gf") if last else None
        NBUF = 2
        hbp = [c.sb([128, TB, D]) for _ in range(NBUF)]
        hb = [[hbp[s_][:, u_, :] for u_ in range(TB)] for s_ in range(NBUF)]
        nb = [[c.sb([128, D], BF16) for _ in range(TB)] for _ in range(NBUF)]
        nT = [[c.sb([128, 8, 128], BF16) for _ in range(TB)] for _ in range(NBUF)]
        aT = [c.sb([128, 128 * TB], BF16) for _ in range(22)]
        junk = c.sb([128, D], BF16)
        sil = [c.sb([128, 128 * TB]) for _ in range(3)]
        st = [c.sb([128, 8]) for _ in range(NBUF)]
        obp = c.sb([128, TB, D]) if last else None
        ob = [obp[:, u_, :] for u_ in range(TB)] if last else None

        def stage_load(i, s):
            c.dma("sp", hbp[s], src.pair(0), "ffn_h%d" % s)

        def stage_norm(i, s):
            for u in range(TB):
                self.rmsnorm_to_bf16(hb[s][u], g2, nb[s][u], junk, st[s][:, u:u + 1], st[s][:, 2 + u:3 + u],
                                     st[s][:, 4 + u:5 + u])
                self.transpose_bf16(nT[s][u], nb[s][u], 8)

        def stage_main(i, s):
            for f in range(22):
                bk = c.bank()
                for half, w in ((0, wg), (1, wu)):
                    for u in range(TB):
                        for k in range(8):
                            c.mm(bk[:, half * 256 + u * 128: half * 256 + (u + 1) * 128],
                                 w[:, k, f * 128:(f + 1) * 128], nT[s][u][:, k, :], start=(k == 0), stop=(k == 7))
                sl = sil[f % 3]
                c.act(sl, bk[:, 0:128 * TB], AF.Silu)
                c.tt("dve", aT[f], sl, bk[:, 256:256 + 128 * TB], ALU.mult)
            for u in range(TB):
                for half in range(2):
                    bk = c.bank()
                    for f in range(22):
                        c.mm(bk, aT[f][:, u * 128:(u + 1) * 128], wd[:, f, half * 512:(half + 1) * 512],
                             start=(f == 0), stop=(f == 21))
                    hs = hb[s][u][:, half * 512:(half + 1) * 512]
                    c.tt("dve", hs, bk, hs, ALU.add)
                if last:
                    self.final_norm(hb[s][u], gf, ob[u], junk, st[s][:, 6:7], st[s][:, 7:8], st[s][:, 5:6])
            if last:
                c.dma("act", dst.pair(0), obp, "ffn_f")
            else:
                c.dma("act", dst.pair(0), hbp[s], "ffn_o%d" % s)

        c.S.new_segment(loop=(0, ntile))
        stage_load("L", 0)
        stage_norm("L", 0)
        stage_main("L", 0)
        c.S.new_segment()

    def final_norm(self, hb, gf, ob, junk, ss, tmp, rs):
        c = self.c
        c.act(junk, hb, AF.Square, accum=ss)
        c.rstd(rs, ss, 1.0 / D, RMS_EPS, self.nhalf[:, 0:1], tmp)
        c.stt(ob, hb, rs, gf, ALU.mult, ALU.mult)

    def pass_swa(self, li, layer, src, dst):
        c = self.c
        NT = self.NT
        wq = c.sb([128, 8, 1024], BF16)
        wk = c.sb([128, 8, 4, 128], BF16)
        wv = c.sb([128, 8, 256], BF16)
        wo = c.sb([128, 8, 1024], BF16)
        Wd = self.din("swa_w_qkv")[li]
        self.load_w(wq, Wd[:, 0:1024], "wq", 8)
        for k in range(8):
            for dup in range(2):
                c.dma("pool", wk[:, k, :, dup * 64:(dup + 1) * 64],
                      V(Wd[k * 128:(k + 1) * 128, 1024:1280].rearrange("p (a b) -> p a b", b=64)), "wk")
        self.load_w(wv, Wd[:, 1280:1536], "wv", 8)
        self.load_w(wo, self.din("swa_w_o")[li], "wo", 8)
        g1 = self.bcast_load(self.din("norm1_g")[layer], D, "g1")
        bq = self.col_load(self.din("swa_b_qkv")[li, 0:1024], 8, "bq")
        bk = c.sb([128, 4])
        for dup in range(2):
            c.S.dma("sp", (lambda o, i: (lambda e, it: e.dma_start(out=o, in_=i, allow_slow_non_contiguous=True)))(
                bk.ap[dup * 64:(dup + 1) * 64, :], self.din("swa_b_qkv")[li, 1024:1280].rearrange("(a p) -> p a", p=64)),
                "bk", writes=[bk.res])
        bv = self.bcast_load(self.din("swa_b_qkv")[li, 1280:1536], 256, "bv")
        bo = self.bcast_load(self.din("swa_b_o")[li], D, "bo")
        snk = self.bcast_load(self.din("swa_sinks")[li], 16, "snk")
        mtab = self.ct_load("swa_m", "ct_swa")
        masks = [mtab[:, 256 * m: 256 * (m + 1)] for m in range(3)]

        NBUF = 2
        hbp = c.sb([128, 2, D])
        hb = [hbp[:, s_, :] for s_ in range(NBUF)]
        nb = [c.sb([128, D], BF16) for _ in range(NBUF)]
        nT = [c.sb([128, 8, 128], BF16) for _ in range(NBUF)]
        qT = [c.sb([128, 8, 128], BF16) for _ in range(NBUF)]
        kT = c.sb([128, 4, 256], BF16)
        kTh = [kT.sub((slice(None), slice(None), slice(h * 128, (h + 1) * 128))) for h in range(2)]
        vb = [c.sb([128, 256], BF16) for _ in range(2)]
        junk = c.sb([128, D], BF16)
        st = [c.sb([128, 8]) for _ in range(NBUF)]
        sm = [c.sb([128, 4, 256]) for _ in range(2)]
        pb = [c.sb([128, 4, 256], BF16) for _ in range(2)]
        pT = [c.sb([128, 8, 128], BF16) for _ in range(4)]
        mx = [c.sb([128, 16]) for _ in range(NBUF)]
        nm = [c.sb([128, 16]) for _ in range(NBUF)]
        rsum = [c.sb([128, 16]) for _ in range(NBUF)]
        es = [c.sb([128, 16]) for _ in range(NBUF)]
        rec = [c.sb([128, 16]) for _ in range(NBUF)]
        ob = [c.sb([128, D], BF16) for _ in range(NBUF)]
        oT = [c.sb([128, 8, 128], BF16) for _ in range(NBUF)]
        c.memset("dve", kT, 0.0)
        for v_ in vb:
            c.memset("dve", v_, 0.0)

        def stage_pre(i, s):
            if isinstance(i, int):
                c.dma("sp", hb[s], src(i), "swa_h%d" % s)
            elif s == 0:
                c.dma("sp", hbp, src.pair(1), "swa_hp")
            self.rmsnorm_to_bf16(hb[s], g1, nb[s], junk, st[s][:, 0:1], st[s][:, 1:2], st[s][:, 2:3])
            self.transpose_bf16(nT[s], nb[s], 8)
            c.tt("dve", hb[s], hb[s], bo, ALU.add)

        def stage_main(i, s, first=False):
            blk = s
            for g in range(2):
                bkq = c.bank()
                for j in range(4):
                    ft = g * 4 + j
                    for k in range(8):
                        c.mm(bkq[:, j * 128:(j + 1) * 128], wq[:, k, ft * 128:(ft + 1) * 128], nT[s][:, k, :],
                             start=(k == 0), stop=(k == 7))
                for j in range(4):
                    ft = g * 4 + j
                    c.act(qT[s][:, ft, :], bkq[:, j * 128:(j + 1) * 128], AF.Identity, bias=bq[:, ft:ft + 1])
            bkk = c.bank()
            for kv in range(4):
                for k in range(8):
                    c.mm(bkk[:, kv * 128:(kv + 1) * 128], wk[:, k, kv, :], nT[s][:, k, :], start=(k == 0), stop=(k == 7))
            for kv in range(4):
                c.ts("dve", kTh[blk][:, kv, :], bkk[:, kv * 128:(kv + 1) * 128], bk[:, kv:kv + 1], None, ALU.add)
            bkv = c.bank()
            for k in range(8):
                c.mm(bkv[:, 0:256], nT[s][:, k, :], wv[:, k, :], start=(k == 0), stop=(k == 7))
            c.tt("dve", vb[blk], bkv[:, 0:256], bv, ALU.add)
            mask = masks[2] if first else masks[blk]
            for g in range(4):
                bks2 = [c.bank(), c.bank()]
                for j in range(4):
                    hq = g * 4 + j
                    r0 = (hq % 2) * 64
                    c.mm(bks2[hq % 2][:, (j // 2) * 256:(j // 2 + 1) * 256], qT[s][r0:r0 + 64, hq // 2, :],
                         V(kT.ap[r0:r0 + 64, g, :], kTh[0].res), start=True, stop=True, extra_r=[kTh[1].res])
                for j in range(4):
                    hq = g * 4 + j
                    c.stt(sm[g % 2][:, j, :], bks2[hq % 2][:, (j // 2) * 256:(j // 2 + 1) * 256], 0.125, mask,
                          ALU.mult, ALU.add)
                c.red(mx[s][:, g * 4:(g + 1) * 4], sm[g % 2], ALU.max)
                c.tt("dve", mx[s][:, g * 4:(g + 1) * 4], mx[s][:, g * 4:(g + 1) * 4], snk[:, g * 4:(g + 1) * 4], ALU.max)
                c.ts("dve", nm[s][:, g * 4:(g + 1) * 4], mx[s][:, g * 4:(g + 1) * 4], -1.0, None, ALU.mult)
                for j in range(4):
                    hq = g * 4 + j
                    c.act(pb[g % 2][:, j, :], sm[g % 2][:, j, :], AF.Exp, bias=nm[s][:, hq:hq + 1],
                          accum=rsum[s][:, hq:hq + 1])
                bkt = c.bank().bitcast(BF16)
                for j in range(4):
                    for half in range(2):
                        c.tr(bkt[:, (j * 2 + half) * 128:(j * 2 + half + 1) * 128],
                             pb[g % 2][:, j, half * 128:(half + 1) * 128], self.identb)
                c.copy("act", pT[g], bkt.re("p (a b) -> p a b", b=128))
            c.tt("dve", es[s], snk, mx[s], ALU.subtract)
            c.act(es[s], es[s], AF.Exp)
            c.tt("dve", es[s], es[s], rsum[s], ALU.add)
            c.recip(rec[s], es[s])
            for g2 in range(2):
                bko = c.bank()
                for j in range(8):
                    hq = g2 * 8 + j
                    g = hq // 4
                    for half in range(2):
                        vsrc = vb[half]
                        c.mm(bko[:, j * 64:(j + 1) * 64], pT[g][:, (hq % 4) * 2 + half, :], vsrc[:, g * 64:(g + 1) * 64],
                             start=(half == 0), stop=(half == 1))
                c.tt("dve", ob[s][:, g2 * 512:(g2 + 1) * 512].re("p (a b) -> p a b", b=64),
                     bko.re("p (a b) -> p a b", b=64),
                     rec[s][:, g2 * 8:(g2 + 1) * 8].un(2).bc([128, 8, 64]), ALU.mult)
            self.transpose_bf16(oT[s], ob[s], 8)
            for half in range(2):
                bkp = c.bank()
                for k in range(8):
                    c.mm(bkp, oT[s][:, k, :], wo[:, k, half * 512:(half + 1) * 512], start=(k == 0), stop=(k == 7))
                hs = hb[s][:, half * 512:(half + 1) * 512]
                c.tt("dve", hs, bkp, hs, ALU.add)
            if isinstance(i, int):
                c.dma("act", dst(i), hb[s], "swa_o%d" % s)
            elif s == 1:
                c.dma("act", dst.pair(1), hbp, "swa_op")

        stage_pre(0, 0)
        stage_pre(1, 1)
        stage_main(0, 0, first=True)
        stage_main(1, 1)
        if NT > 2:
            c.S.new_segment(loop=(0, NT // 2 - 1))
            stage_pre("L", 0)
            stage_pre("L", 1)
            stage_main("L", 0)
            stage_main("L", 1)
            c.S.new_segment()

    def pass_even(self, li, layer, src, dst):
        c = self.c
        NT = self.NT
        f3 = lambda v, b=128: v.re("p (a b) -> p a b", b=b)
        w_in = c.sb([128, 8, EVEN_IN], BF16)
        w_out = c.sb([128, 8, D], BF16)
        self.load_w(w_in, self.din("even_w_in")[li], "w_in", 8)
        self.load_w(w_out, self.din("even_w_out")[li], "w_out", 8)
        W1 = c.sb([128, 512], BF16)
        W2 = c.sb([128, 512], BF16)
        GU = c.sb([128, 512], BF16)
        c.memset("dve", W1, 0.0)
        c.memset("dve", W2, 0.0)
        c.dma("pool", W1[0:64, :], V(self.din("rwkv_w_up")[li]), "W1")
        c.dma("pool", W2[64:128, :], V(self.din("rwkv_a_up")[li]), "W2")
        c.dma("pool", GU, V(self.din("rwkv_g_up")[li]), "GU")
        g1 = self.bcast_load(self.din("norm1_g")[layer], D, "g1")
        mu = self.col_load(self.din("rwkv_mu")[li], 14, "mu")
        w0 = self.col_load(self.din("rwkv_w0")[li], 4, "w0")
        a0 = self.col_load(self.din("rwkv_a0")[li], 4, "a0")
        k_k = self.col_load(self.din("rwkv_k_k")[li], 4, "k_k")
        k_a = self.col_load(self.din("rwkv_k_a")[li], 4, "k_a")
        r_k = self.col_load(self.din("rwkv_r_k")[li], 4, "r_k")
        ln_g = self.col_load(self.din("rwkv_ln_g")[li], 4, "ln_g")
        ln_b = self.col_load(self.din("rwkv_ln_b")[li], 4, "ln_b")
        omu = c.sb([128, 14])
        c.ts("dve", omu, mu, -1.0, 1.0, ALU.mult, ALU.add)
        omka = c.sb([128, 4])
        c.ts("dve", omka, k_a, -1.0, 1.0, ALU.mult, ALU.add)
        c.ts("dve", w0, w0, -1.0, None, ALU.mult)
        c.ts("dve", a0, a0, -1.0, None, ALU.mult)
        DT = self.ct_load("ret_DT", "ct_DT")
        qdec = self.ct_load("ret_qdec", "ct_qdec")
        kdec = self.ct_load("ret_kdec", "ct_kdec")
        gC = self.ct_load("ret_gC", "ct_gC")
        mNM = self.ct_load("rw_mNM", "ct_mNM")
        mL = self.ct_load("rw_mL", "ct_mL")
        zeros = c.sb([128, 128])
        c.memset("dve", zeros, 0.0)
        nh = self.nhalf
        CW = -math.exp(-0.5)

        NBUF = 2
        hbp = c.sb([128, 2, D])
        hb = [hbp[:, s_, :] for s_ in range(NBUF)]
        rpp = c.sb([128, 2, 256])
        rp = [rpp[:, s_, :] for s_ in range(NBUF)]
        rope_t = DT_(self.rope_d)
        nb = c.sb([128, D], BF16)
        nT = [c.sb([128, 8, 130], BF16) for _ in range(NBUF)]
        mixT = [c.sb([128, 8, 128], BF16) for _ in range(NBUF)]
        st = [c.sb([128, 8]) for _ in range(NBUF)]
        fa = c.sb([128, 512]); fb = c.sb([128, 512]); th = c.sb([128, 512]); sg2 = c.sb([128, 512])
        qr = c.sb([128, 512], BF16); kr = c.sb([128, 512], BF16); kd = c.sb([128, 512], BF16); vb = c.sb([128, 512], BF16)
        qTp = c.sb([128, 2, 4, 128], BF16); kTr = c.sb([128, 4, 128], BF16); qdT = c.sb([128, 4, 128], BF16)
        PT = c.sb([128, 8, 128], BF16)
        S32 = c.sb([128, 4, 64]); Spad = c.sb([128, 2, 4, 64], BF16)
        ro = c.sb([128, 512], BF16)
        gst = c.sb([128, 64])
        c.memset("dve", qTp, 0.0); c.memset("dve", S32, 0.0); c.memset("dve", Spad, 0.0)
        B = [c.sb([128, 512]) for _ in range(16)]
        AR = c.sb([128, 4, 2, 128])
        Btp = c.sb([128, 2, 4, 128]); Ktp = c.sb([128, 2, 4, 128])
        wa = c.sb([128, 128]); gl = c.sb([128, 128]); tg = c.sb([128, 128])
        wab = c.sb([128, 128], BF16); sglb = c.sb([128, 128], BF16)
        Stp = c.sb([128, 2, 4, 64])
        gst2 = c.sb([128, 64])
        c.memset("dve", Btp, 0.0); c.memset("dve", Ktp, 0.0); c.memset("dve", Stp, 0.0)
        HS = 4
        hNM1 = [c.sb([128, 256]) for _ in range(HS)]
        hNM2 = [c.sb([128, 256]) for _ in range(HS)]
        hPTa = [c.sb([128, 128]) for _ in range(HS)]
        hPb = [c.sb([128, 128]) for _ in range(HS)]
        hPTb = [c.sb([128, 128]) for _ in range(HS)]
        hQ = [[c.sb([128, 128]) for _ in range(2)] for _ in range(HS)]
        hX = [c.sb([128, 64]) for _ in range(HS)]
        ybank = c.banks[7]

        def gnorm(bank, osq, t1, gs, eps, stage=False):
            if stage:
                c.copy("dve", t1, bank)
                bank = t1
            b3 = f3(bank, 64)
            c.red(gs[:, 0:8], b3, ALU.add)
            c.act(osq, bank, AF.Square)
            c.red(gs[:, 8:16], f3(osq, 64), ALU.add)
            c.ts("dve", gs[:, 16:24], gs[:, 0:8], 1.0 / 64, None, ALU.mult)
            c.tt("dve", gs[:, 24:32], gs[:, 16:24], gs[:, 16:24], ALU.mult)
            c.stt(gs[:, 32:40], gs[:, 8:16], 1.0 / 64, gs[:, 24:32], ALU.mult, ALU.subtract)
            c.rstd(gs[:, 40:48], gs[:, 32:40], 1.0, eps, nh[:, 0:8], gs[:, 48:56])
            c.tt("dve", f3(t1, 64), b3, gs[:, 16:24].un(2).bc([128, 8, 64]), ALU.subtract)
            c.tt("dve", f3(t1, 64), f3(t1, 64), gs[:, 40:48].un(2).bc([128, 8, 64]), ALU.mult)

        def stage_pre(i, s, first=False):
            if isinstance(i, int):
                c.dma("sp", hb[s], src(i), "ev_h%d" % s)
                c.dma("sp", rp[s], rope_t(i), "ev_rp%d" % s)
            elif s == 0:
                c.dma("sp", hbp, src.pair(1), "ev_hp")
                c.dma("sp", rpp, rope_t.pair(1), "ev_rpp")
            self.rmsnorm_to_bf16(hb[s], g1, nb, nb, st[s][:, 0:1], st[s][:, 1:2], st[s][:, 2:3])
            self.transpose_bf16(nT[s][:, :, 2:130], nb, 8)
            if first:
                c.memset("dve", nT[s][:, :, 0:2], 0.0)
            else:
                c.copy("dve", nT[s][:, :, 1:2], nT[1 - s][:, :, 129:130])

        def stage_ret(i, s):
            r = rp[s]
            for cg in range(4):
                bk = c.bank()
                for k in range(8):
                    c.mm(bk, nT[s][:, k, 2:130], w_in[:, k, cg * 512:(cg + 1) * 512], start=(k == 0), stop=(k == 7))
                b3 = f3(bk, 64)
                if cg < 2:
                    ro_ = cg * 128
                    dst_ = qr if cg == 0 else kr
                    c.tt("dve", f3(fa, 64), b3, r[:, ro_:ro_ + 64].un(1).bc([128, 8, 64]), ALU.mult)
                    c.tt("dve", f3(fb, 64)[:, :, 0:32], b3[:, :, 32:64], r[:, ro_ + 64:ro_ + 96].un(1).bc([128, 8, 32]), ALU.mult)
                    c.tt("dve", f3(fb, 64)[:, :, 32:64], b3[:, :, 0:32], r[:, ro_ + 96:ro_ + 128].un(1).bc([128, 8, 32]), ALU.mult)
                    c.tt("dve", dst_, fa, fb, ALU.add)
                    if cg == 1:
                        c.tt("dve", f3(kd, 64), f3(kr, 64), kdec.un(2).bc([128, 8, 64]), ALU.mult)
                elif cg == 2:
                    c.copy("act", vb, bk)
                else:
                    c.act(th, bk, AF.Exp, scale=-1.0)
                    c.ts("dve", th, th, 1.0, None, ALU.add)
                    c.recip(th, th)
                    c.tt("dve", sg2, bk, th, ALU.mult)
            bq_ = c.bank().bitcast(BF16)
            for p in range(4):
                c.tr(bq_[:, p * 128:(p + 1) * 128], qr[:, p * 128:(p + 1) * 128], self.identb)
                c.tr(bq_[:, 512 + p * 128:512 + (p + 1) * 128], kr[:, p * 128:(p + 1) * 128], self.identb)
            bq3 = f3(bq_[:, 0:512])
            c.tt("dve", qdT, bq3, f3(qdec), ALU.mult)
            c.copy("act", qTp[0:64, 0, :, :], bq3[0:64, :, :])
            c.copy("act", qTp[64:128, 1, :, :], bq3[64:128, :, :])
            c.copy("act", kTr, f3(bq_[:, 512:1024]))
            for g in range(2):
                bs = c.bank()
                for j in range(4):
                    h = g * 4 + j
                    c.mm(bs[:, j * 128:(j + 1) * 128], kTr[:, h // 2, :], qTp[:, h % 2, h // 2, :])
                c.tt("dve", PT[:, g * 4:(g + 1) * 4, :], f3(bs), f3(DT)[:, g * 4:(g + 1) * 4, :], ALU.mult)
            bo_ = c.bank()
            for h in range(8):
                c.mm(bo_[:, h * 64:(h + 1) * 64], PT[:, h, :], vb[:, h * 64:(h + 1) * 64], start=True, stop=False)
                c.mm(bo_[:, h * 64:(h + 1) * 64], qdT[:, h // 2, :], Spad[:, h % 2, h // 2, :], start=False, stop=True)
            bS = c.bank()
            for p in range(4):
                c.mm(bS[:, p * 128:(p + 1) * 128], kd[:, p * 128:(p + 1) * 128], vb[:, p * 128:(p + 1) * 128])
            bS3 = f3(bS)
            c.tt("dve", S32, S32, gC.un(2).bc([128, 4, 64]), ALU.mult)
            c.tt("dve", S32[0:64], S32[0:64], bS3[0:64, :, 0:64], ALU.add)
            c.tt("dve", S32[64:128], S32[64:128], bS3[64:128, :, 64:128], ALU.add)
            c.copy("act", Spad[0:64, 0], S32[0:64])
            c.copy("act", Spad[64:128, 1], S32[64:128])
            gnorm(bo_, fa, fb, gst, RET_GN_EPS)
            c.tt("dve", ro, fb, sg2, ALU.mult)
            self.transpose_bf16(mixT[s][:, 0:4, :], ro, 4)

        def stage_rwkv(i, s):
            rT, kraw, vT, lw, al, gT, L, E1, E2, E3, E4, kkf, tmp, inv, kp = B[1:16]
            dests = [rT, kraw, vT]
            for g0 in range(0, 14, 3):
                bk = c.bank()
                fts = list(range(g0, min(g0 + 3, 14)))
                for j, ft in enumerate(fts):
                    for k in range(8):
                        c.mm(bk[:, j * 129:(j + 1) * 129], w_in[:, k, 2048 + ft * 128:2048 + (ft + 1) * 128],
                             nT[s][:, k, 1:130], start=(k == 0), stop=(k == 7))
                for j, ft in enumerate(fts):
                    if ft < 12:
                        d_ = f3(dests[ft // 4])[:, ft % 4, :]
                    elif ft == 12:
                        d_ = wa
                    else:
                        d_ = gl
                    c.act(tg, bk[:, j * 129:j * 129 + 128], AF.Identity, scale=mu[:, ft:ft + 1])
                    c.stt(d_, bk[:, j * 129 + 1:j * 129 + 129], omu[:, ft:ft + 1], tg, ALU.mult, ALU.add)
            c.act(tg[0:64, :], wa[0:64, :], AF.Exp, scale=-2.0)
            c.ts("dve", tg[0:64, :], tg[0:64, :], 1.0, None, ALU.add)
            c.recip(tg[0:64, :], tg[0:64, :])
            c.ts("dve", wab[0:64, :], tg[0:64, :], 2.0, -1.0, ALU.mult, ALU.add)
            c.copy("act", wab[64:128, :], wa[64:128, :])
            c.act(tg, gl, AF.Exp, scale=-1.0)
            c.ts("dve", tg, tg, 1.0, None, ALU.add)
            c.recip(tg, tg)
            c.copy("act", sglb, tg)
            bw = c.bank(); ba = c.bank(); bg = c.bank()
            for ft in range(4):
                c.mm(bw[:, ft * 128:(ft + 1) * 128], W1[:, ft * 128:(ft + 1) * 128], wab)
            for ft in range(4):
                c.mm(ba[:, ft * 128:(ft + 1) * 128], W2[:, ft * 128:(ft + 1) * 128], wab)
            for ft in range(4):
                c.mm(bg[:, ft * 128:(ft + 1) * 128], GU[:, ft * 128:(ft + 1) * 128], sglb)
            for ft in range(4):
                c.act(f3(lw)[:, ft, :], bw[:, ft * 128:(ft + 1) * 128], AF.Exp, bias=w0[:, ft:ft + 1], scale=-1.0)
                c.act(f3(al)[:, ft, :], ba[:, ft * 128:(ft + 1) * 128], AF.Exp, bias=a0[:, ft:ft + 1], scale=-1.0)
            c.ts("dve", lw, lw, 1.0, None, ALU.add)
            c.recip(lw, lw)
            c.ts("dve", lw, lw, CW, None, ALU.mult)
            c.ts("dve", al, al, 1.0, None, ALU.add)
            c.recip(al, al)
            c.copy("act", gT, bg)
            for ft in range(4):
                c.scan(f3(L)[:, ft, :], f3(lw)[:, ft, :], zeros, ALU.add, ALU.add)
            c.act(E1, L, AF.Exp)
            c.act(E2, L, AF.Exp, scale=-1.0)
            c.tt("dve", E3, L, lw, ALU.subtract)
            c.act(E3, E3, AF.Exp)
            for ft in range(4):
                c.act(f3(E4)[:, ft, :], f3(L)[:, ft, :], AF.Exp, bias=f3(L)[:, ft, 127:128], scale=-1.0)
            c.tt("dve", f3(kkf), f3(kraw), k_k.un(2).bc([128, 4, 128]), ALU.mult)
            c.act(tmp, kkf, AF.Square)
            bss = c.bank()
            c.mm(bss, self.bones, tmp)
            c.ts("dve", tmp, bss, 1e-19, None, ALU.max)
            c.act(tmp, tmp, AF.Ln)
            c.act(inv, tmp, AF.Exp, scale=-0.5)
            c.tt("dve", kkf, kkf, inv, ALU.mult)
            kk = kkf
            c.tt("dve", f3(tmp), f3(al), k_a.un(2).bc([128, 4, 128]), ALU.mult)
            c.tt("dve", f3(tmp), f3(tmp), omka.un(2).bc([128, 4, 128]), ALU.add)
            c.tt("dve", kp, kraw, tmp, ALU.mult)
            b_ = inv
            c.tt("dve", b_, kk, al, ALU.mult)
            c.stt(AR[:, :, 0, :], f3(kk), -1.0, f3(E3), ALU.mult, ALU.mult)
            c.tt("dve", AR[:, :, 1, :], f3(rT), f3(E1), ALU.mult)
            c.tt("dve", Btp[0:64, 0], f3(b_)[0:64], f3(E2)[0:64], ALU.mult)
            c.tt("dve", Btp[64:128, 1], f3(b_)[64:128], f3(E2)[64:128], ALU.mult)
            c.tt("dve", Ktp[0:64, 0], f3(kp)[0:64], f3(E2)[0:64], ALU.mult)
            c.tt("dve", Ktp[64:128, 1], f3(kp)[64:128], f3(E2)[64:128], ALU.mult)
            Bp = E3
            Kp = E2
            c.tt("dve", Bp, b_, E4, ALU.mult)
            c.tt("dve", Kp, kp, E4, ALU.mult)
            c.tt("dve", tmp, rT, kp, ALU.mult)
            c.tt("dve", f3(tmp), f3(tmp), r_k.un(2).bc([128, 4, 128]), ALU.mult)
            bbn = c.bank()
            c.mm(bbn, self.bones, tmp)
            bonus = lw
            c.tt("dve", bonus, bbn, vT, ALU.mult)
            Bp_tok, Kp_tok, V_tok = L, E4, kkf
            for src_, dst_ in ((Bp, Bp_tok), (Kp, Kp_tok), (vT, V_tok)):
                bt_ = c.bank()
                for ft in range(4):
                    c.tr(bt_[:, ft * 128:(ft + 1) * 128], f3(src_)[:, ft, :], self.identf)
                c.copy("act", dst_, bt_)
            Us = inv
            for wave in range(2):
                heads = list(range(wave * 4, wave * 4 + 4))
                for q_, h in enumerate(heads):
                    p = h // 2
                    par = h % 2
                    b1 = c.bank()
                    c.mm(b1[:, 0:256], Btp[:, par, p, :], AR[:, p, :, :].re("p a b -> p (a b)"))
                    c.mm(b1[:, 256:512], Ktp[:, par, p, :], AR[:, p, :, :].re("p a b -> p (a b)"))
                    b2 = c.bank()
                    c.mm(b2[:, 0:128], AR[:, p, 0, :], Btp[:, par, p, :])
                    c.tt("dve", hNM1[q_], b1[:, 0:256], mNM, ALU.mult)
                    c.tt("dve", hNM2[q_], b1[:, 256:512], mNM, ALU.mult)
                    c.tt("dve", hPTa[q_], b2[:, 0:128], mL, ALU.mult)
                    c.tt("dve", hQ[q_][0], hNM1[q_][:, 0:128], self.identf, ALU.add)
                cur = [(hNM1[q_][:, 0:128], hPTa[q_]) for q_ in range(4)]
                nxt = [(hPb[q_], hPTb[q_]) for q_ in range(4)]
                for lev in range(1, 7):
                    bks = []
                    for q_ in range(4):
                        P_, PT_ = cur[q_]
                        bk = c.bank()
                        c.mm(bk[:, 128:256], P_, PT_)
                        if lev < 6:
                            c.mm(bk[:, 0:128], PT_, P_)
                        bks.append(bk)
                    for q_ in range(4):
                        Pn, PTn = nxt[q_]
                        c.copy("dve", PTn, bks[q_][:, 128:256])
                        if lev < 6:
                            c.copy("dve", Pn, bks[q_][:, 0:128])
                    bks2 = []
                    for q_ in range(4):
                        Pn, PTn = nxt[q_]
                        bk = c.bank()
                        c.mm(bk[:, 0:128], PTn, hQ[q_][(lev - 1) % 2])
                        bks2.append(bk)
                    for q_ in range(4):
                        c.tt("dve", hQ[q_][lev % 2], bks2[q_][:, 0:128], hQ[q_][(lev - 1) % 2], ALU.add)
                    cur, nxt = nxt, cur
                for q_, h in enumerate(heads):
                    p = h // 2
                    par = h % 2
                    Qf = hQ[q_][0]
                    bx = c.bank()
                    c.mm(bx[:, 0:64], AR[:, p, 0, :], Stp[:, par, p, :], start=True, stop=False)
                    c.mm(bx[:, 0:64], hNM2[q_][:, 0:128], V_tok[:, h * 64:(h + 1) * 64], start=False, stop=True)
                    c.copy("dve", hX[q_], bx[:, 0:64])
                    c.mm(bx[:, 64:128], Qf, hX[q_])
                    c.copy("dve", Us[:, h * 64:(h + 1) * 64], bx[:, 64:128])
                    yb = ybank[:, h * 64:(h + 1) * 64]
                    c.mm(yb, AR[:, p, 1, :], Stp[:, par, p, :], start=True, stop=False)
                    c.mm(yb, hNM1[q_][:, 128:256], Us[:, h * 64:(h + 1) * 64], start=False, stop=False)
                    c.mm(yb, hNM2[q_][:, 128:256], V_tok[:, h * 64:(h + 1) * 64], start=False, stop=True)
                    if par == 1:
                        bsu = c.bank()
                        c.mm(bsu[:, 0:128], Bp_tok[:, p * 128:(p + 1) * 128], Us[:, p * 128:(p + 1) * 128], start=True, stop=False)
                        c.mm(bsu[:, 0:128], Kp_tok[:, p * 128:(p + 1) * 128], V_tok[:, p * 128:(p + 1) * 128], start=False, stop=True)
                        c.stt(Stp[0:64, 0, p, :], Stp[0:64, 0, p, :], f3(E1)[0:64, p, 127:128], bsu[0:64, 0:64], ALU.mult, ALU.add)
                        c.stt(Stp[64:128, 1, p, :], Stp[64:128, 1, p, :], f3(E1)[64:128, p, 127:128], bsu[64:128, 64:128], ALU.mult, ALU.add)
            yn, osq = tmp, kp
            gnorm(ybank, osq, yn, gst2, RWKV_GN_EPS, stage=True)
            bt_ = c.bank()
            for ft in range(4):
                c.tr(bt_[:, ft * 128:(ft + 1) * 128], yn[:, ft * 128:(ft + 1) * 128], self.identf)
            o1 = osq
            c.tt("dve", f3(o1), f3(bt_), ln_g.un(2).bc([128, 4, 128]), ALU.mult)
            c.tt("dve", f3(o1), f3(o1), ln_b.un(2).bc([128, 4, 128]), ALU.add)
            c.tt("dve", o1, o1, bonus, ALU.add)
            c.tt("dve", mixT[s][:, 4:8, :], f3(o1), f3(gT), ALU.mult)

        def stage_out(i, s):
            for half in range(2):
                bkp = c.bank()
                for k in range(8):
                    c.mm(bkp, mixT[s][:, k, :], w_out[:, k, half * 512:(half + 1) * 512], start=(k == 0), stop=(k == 7))
                hs = hb[s][:, half * 512:(half + 1) * 512]
                c.tt("dve", hs, bkp, hs, ALU.add)
            if isinstance(i, int):
                c.dma("act", dst(i), hb[s], "ev_o%d" % s)
            elif s == 1:
                c.dma("act", dst.pair(1), hbp, "ev_op")

        def tile(i, s, first=False):
            stage_pre(i, s, first)
            stage_ret(i, s)
            stage_rwkv(i, s)
            stage_out(i, s)

        tile(0, 0, first=True)
        tile(1, 1)
        if NT > 2:
            c.S.new_segment(loop=(0, NT // 2 - 1))
            tile("L", 0)
            tile("L", 1)
            c.S.new_segment()


def default_plan():
    plan = []
    for layer in range(4):
        plan.append(("E" if layer % 2 == 0 else "O", layer, False))
        plan.append(("B", layer, layer == 3))
    return plan


_CACHE = {}


def run_prog(T, plan, per_core_inputs):
    key = (T, tuple(plan))
    if key not in _CACHE:
        p = Prog(T, plan)
        _CACHE[key] = (p.build(), set(p.dins))
    nc, used = _CACHE[key]
    ctab = make_ctab()
    rope = make_rope(T)
    maps = []
    for m in per_core_inputs:
        m = dict(m)
        m["ctab"] = ctab
        m["rope"] = rope
        maps.append({k: v for k, v in m.items() if k in used})
    res = run_bass_kernel_spmd(nc, maps, core_ids=list(range(len(maps))))
    return [r["y"] for r in res.results]


def kernel(**inputs):
    x = np.asarray(inputs["x"], np.float32)
    B, T, _ = x.shape
    shared = {}
    for k, v in inputs.items():
        if k == "x":
            continue
        a = np.ascontiguousarray(np.asarray(v, np.float32))
        if k in ("rwkv_r_k", "rwkv_ln_g", "rwkv_ln_b"):
            a = a.reshape(a.shape[0], -1)
        shared[k] = a
    maps = []
    for b in range(B):
        m = dict(shared)
        m["x"] = np.ascontiguousarray(x[b])
        maps.append(m)
    outs = run_prog(T, default_plan(), maps)
    return np.stack(outs, axis=0).astype(np.float32)
```

```python
import math
from contextlib import ExitStack

import numpy as np
import concourse.bass as bass
import concourse.mybir as mybir
from concourse.bass_utils import run_bass_kernel_spmd

F32 = mybir.dt.float32
BF16 = mybir.dt.bfloat16
AF = mybir.ActivationFunctionType
ALU = mybir.AluOpType
AX = mybir.AxisListType

D = 1024
HD = 64
DFF = 2816
EVEN_IN = 3840
SWA_QKV = 1536
RMS_EPS = 1e-6
RET_GN_EPS = 1e-6
RWKV_GN_EPS = 64e-5
NEG = -30000.0

ENGS = ("pe", "act", "dve", "pool", "sp")


class Op:
    __slots__ = ("eng", "fn", "deps", "signal", "seq", "is_dma", "dkey", "dval", "epoch")

    def __init__(self, eng, fn, deps, is_dma=False):
        self.eng = eng
        self.fn = fn
        self.deps = deps
        self.signal = False
        self.seq = 0
        self.is_dma = is_dma
        self.dkey = None
        self.dval = 0
        self.epoch = 0


class Res:
    __slots__ = ("w", "r", "rd")

    def __init__(self):
        self.w = None
        self.r = {}
        self.rd = []


class Sched:
    def __init__(self, nc):
        self.nc = nc
        self.segs = []
        self.epoch = 0
        self.new_segment()

    def new_segment(self, loop=None):
        if self.segs and loop is None and self.segs[-1]["loop"] is None and \
                not any(self.segs[-1]["ops"][e] for e in ENGS):
            return
        self.cur = {"loop": loop, "ops": {e: [] for e in ENGS}, "dk": {}, "dkeng": {}}
        self.segs.append(self.cur)
        self.epoch += 1

    def _live(self, o):
        return o is not None and o.epoch == self.epoch

    def _deps(self, eng, is_dma, reads, writes):
        deps = []
        for r in reads:
            if self._live(r.w):
                deps.append(r.w)
        strict = is_dma or eng != "pe"
        for w in writes:
            x = w.w
            if self._live(x) and (x.is_dma or strict or x.eng != eng):
                deps.append(x)
            for e, x in w.r.items():
                if self._live(x) and (strict or e != eng):
                    deps.append(x)
            for x in w.rd:
                if self._live(x):
                    deps.append(x)
        return deps

    def _commit(self, op, reads, writes):
        op.epoch = self.epoch
        for r in reads:
            if op.is_dma:
                r.rd = [x for x in r.rd if x.epoch == self.epoch]
                r.rd.append(op)
            else:
                r.r[op.eng] = op
        for w in writes:
            w.w = op
            w.r = {}
            w.rd = []
        self.cur["ops"][op.eng].append(op)

    def op(self, eng, fn, reads=(), writes=()):
        o = Op(eng, fn, self._deps(eng, False, reads, writes))
        self._commit(o, reads, writes)
        return o

    def dma(self, eng, fn, key, reads=(), writes=()):
        o = Op(eng, fn, self._deps(eng, True, reads, writes), is_dma=True)
        dk = self.cur["dk"]
        dk[key] = dk.get(key, 0) + 16
        self.cur["dkeng"][key] = eng
        o.dkey = key
        o.dval = dk[key]
        self._commit(o, reads, writes)
        return o

    def emit(self, stack):
        nc = self.nc
        segs = self.segs
        nsl = []
        for seg in segs:
            ns = {}
            for e in ENGS:
                ops = seg["ops"][e]
                for i, o in enumerate(ops):
                    o.seq = i
                last = None
                for o in ops:
                    if not o.is_dma:
                        last = o
                if last is not None:
                    last.signal = True
            for e in ENGS:
                covered = {}
                for o in seg["ops"][e]:
                    best = {}
                    keep = []
                    for d in o.deps:
                        if d.is_dma:
                            keep.append(d)
                            continue
                        if d.eng == e and e == "pe":
                            continue
                        b = best.get(d.eng)
                        if b is None or d.seq > b.seq:
                            best[d.eng] = d
                    for eng2, d in best.items():
                        if covered.get(eng2, -1) >= d.seq:
                            continue
                        covered[eng2] = d.seq
                        d.signal = True
                        keep.append(d)
                    o.deps = keep
            for e in ENGS:
                c = 0
                for o in seg["ops"][e]:
                    if not o.is_dma and o.signal:
                        c += 1
                        o.seq = c
                ns[e] = c + 1
            nsl.append(ns)
        base = []
        dbase = []
        cnt = {e: 0 for e in ENGS}
        dcnt = {}
        for si, seg in enumerate(segs):
            base.append(dict(cnt))
            dbase.append(dict(dcnt))
            n = 1 if seg["loop"] is None else (seg["loop"][1] - seg["loop"][0])
            for e in ENGS:
                cnt[e] += n * nsl[si][e]
            for k, v in seg["dk"].items():
                dcnt[k] = dcnt.get(k, 0) + n * v
        esem = {e: stack.enter_context(nc.semaphore("s_" + e)) for e in ENGS}
        dsem = {}
        for k in dcnt:
            dsem[k] = stack.enter_context(nc.semaphore("d%d" % len(dsem)))
        block = stack.enter_context(nc.Block())

        def run(ename, eng):
            regpool = []

            def getreg(j):
                while len(regpool) <= j:
                    regpool.append(eng.alloc_register("%s_r%d" % (ename, len(regpool))))
                return regpool[j]

            for si, seg in enumerate(segs):
                ops = seg["ops"][ename]
                looped = seg["loop"] is not None
                bases = {}
                for X in ENGS:
                    bases[("e", X)] = (esem[X], base[si][X], nsl[si][X])
                for k in seg["dk"]:
                    bases[("d", k)] = (dsem[k], dbase[si].get(k, 0), seg["dk"][k])
                used = []

                def plan(body_fn):
                    body_fn(True)

                bregs = {}

                def W(kk, local, dry):
                    sem, b, per = bases[kk]
                    if dry:
                        if kk not in used:
                            used.append(kk)
                        return
                    if not looped:
                        eng.wait_ge(sem, b + local)
                    else:
                        rs = getreg(0)
                        eng.reg_add(rs, bregs[kk], local)
                        eng.wait_ge(sem, rs)

                def body(dry):
                    for X in ENGS:
                        if X == ename:
                            continue
                        if not looped and base[si][X] == 0:
                            continue
                        W(("e", X), 0, dry)
                    waited = {}
                    for o in ops:
                        for d in o.deps:
                            if d.is_dma:
                                key = ("d", d.dkey)
                                v = d.dval
                            else:
                                key = ("e", d.eng)
                                v = d.seq
                            if waited.get(key, 0) >= v:
                                continue
                            waited[key] = v
                            W(key, v, dry)
                        if dry:
                            continue
                        ins = o.fn(eng, self._it)
                        if o.is_dma:
                            ins.then_inc(dsem[o.dkey], 16)
                        elif o.signal:
                            ins.then_inc(esem[ename], 1)
                    if nsl[si][ename] > 1:
                        W(("e", ename), nsl[si][ename] - 1, dry)
                    for k, e2 in seg["dkeng"].items():
                        if e2 == ename:
                            W(("d", k), seg["dk"][k], dry)
                    if not dry:
                        eng.sem_inc(esem[ename], 1)

                if not looped:
                    self._it = None
                    body(False)
                else:
                    body(True)
                    for j, kk in enumerate(used):
                        bregs[kk] = getreg(j + 1)
                        eng.reg_mov(bregs[kk], bases[kk][1])
                    with eng.Fori(seg["loop"][0], seg["loop"][1]) as it:
                        self._it = it
                        body(False)
                        for kk in used:
                            eng.reg_add(bregs[kk], bregs[kk], bases[kk][2])

        @block.tensor
        def _(t):
            run("pe", t)

        @block.scalar
        def _(t):
            run("act", t)

        @block.vector
        def _(t):
            run("dve", t)

        @block.gpsimd
        def _(t):
            run("pool", t)

        @block.sync
        def _(t):
            run("sp", t)


class V:
    __slots__ = ("ap", "res")

    def __init__(self, ap, res=None):
        self.ap = ap
        self.res = res if res is not None else Res()

    def __getitem__(self, key):
        return V(self.ap[key], self.res)

    def re(self, pat, **kw):
        return V(self.ap.rearrange(pat, **kw), self.res)

    def bc(self, shape):
        return V(self.ap.to_broadcast(list(shape)), self.res)

    def un(self, axis):
        return V(self.ap.unsqueeze(axis), self.res)

    def sub(self, key):
        return V(self.ap[key], Res())

    def bitcast(self, dt):
        return V(self.ap.bitcast(dt), self.res)


def _res(*vs):
    out = []
    for v in vs:
        if isinstance(v, V):
            out.append(v.res)
    return out


def _ap(v):
    return v.ap if isinstance(v, V) else v


class Ctx:
    def __init__(self, nc, stack, arena_words):
        self.nc = nc
        self.S = Sched(nc)
        self.arena = stack.enter_context(nc.sbuf_tensor("arena", [128, arena_words], F32))
        self.arena_words = arena_words
        self.off = 0
        self.psum = stack.enter_context(nc.psum_tensor("psum", [128, 8, 512], F32))
        self.banks = [V(self.psum[:, b, :]) for b in range(8)]
        self.prr = 0
        self.nrot = 7
        self.consts = {}

    def sb(self, shape, dt=F32):
        n = 1
        for s in shape[1:]:
            n *= s
        n4 = n if dt == F32 else (n + 1) // 2
        n4 = (n4 + 7) // 8 * 8
        if self.off + n4 > self.arena_words:
            raise RuntimeError("arena overflow: need %d words" % (self.off + n4))
        a = self.arena[0:shape[0], self.off:self.off + n4]
        self.off += n4
        if dt != F32:
            a = a.bitcast(dt)
        a = a[:, 0:n]
        if len(shape) == 3:
            a = a.rearrange("p (a b) -> p a b", b=shape[2])
        elif len(shape) == 4:
            a = a.rearrange("p (a b c) -> p a b c", b=shape[2], c=shape[3])
        return V(a)

    def bank(self):
        b = self.banks[self.prr]
        self.prr = (self.prr + 1) % self.nrot
        return b

    def mm(self, out, lhsT, rhs, start=True, stop=True, extra_r=()):
        o, l, r = out.ap, lhsT.ap, rhs.ap
        self.S.op("pe", lambda e, it: e.matmul(o, l, r, start=start, stop=stop),
                  reads=_res(lhsT, rhs) + list(extra_r), writes=_res(out))

    def tr(self, out, in_, ident):
        o, i, d = out.ap, in_.ap, ident.ap
        self.S.op("pe", lambda e, it: e.transpose(o, i, d), reads=_res(in_, ident), writes=_res(out))

    def act(self, out, in_, func, bias=None, scale=None, accum=None, extra_w=()):
        o, i = out.ap, in_.ap
        kw = {}
        if bias is not None:
            kw["bias"] = _ap(bias)
        if scale is not None:
            kw["scale"] = _ap(scale)
        if accum is not None:
            kw["accum_out"] = accum.ap
        self.S.op("act", lambda e, it: e.activation(out=o, in_=i, func=func, **kw),
                  reads=_res(in_, bias, scale), writes=_res(out, accum) + list(extra_w))

    def tt(self, eng, out, in0, in1, op):
        o, a, b = out.ap, in0.ap, in1.ap
        self.S.op(eng, lambda e, it: e.tensor_tensor(out=o, in0=a, in1=b, op=op),
                  reads=_res(in0, in1), writes=_res(out))

    def ts(self, eng, out, in0, s1, s2, op0, op1=None):
        o, a = out.ap, in0.ap
        x1, x2 = _ap(s1), _ap(s2)
        if op1 is None:
            fn = lambda e, it: e.tensor_scalar(out=o, in0=a, scalar1=x1, scalar2=None, op0=op0)
        else:
            fn = lambda e, it: e.tensor_scalar(out=o, in0=a, scalar1=x1, scalar2=x2, op0=op0, op1=op1)
        self.S.op(eng, fn, reads=_res(in0, s1, s2), writes=_res(out))

    def stt(self, out, in0, scalar, in1, op0, op1):
        o, a, b, s = out.ap, in0.ap, in1.ap, _ap(scalar)
        self.S.op("dve", lambda e, it: e.scalar_tensor_tensor(out=o, in0=a, scalar=s, in1=b, op0=op0, op1=op1),
                  reads=_res(in0, scalar, in1), writes=_res(out))

    def copy(self, eng, out, in_):
        o, i = out.ap, in_.ap
        if eng == "act":
            self.S.op("act", lambda e, it: e.activation(out=o, in_=i, func=AF.Copy), reads=_res(in_), writes=_res(out))
        else:
            self.S.op(eng, lambda e, it: e.tensor_copy(out=o, in_=i), reads=_res(in_), writes=_res(out))

    def red(self, out, in_, op):
        o, i = out.ap, in_.ap
        self.S.op("dve", lambda e, it: e.tensor_reduce(out=o, in_=i, axis=AX.X, op=op), reads=_res(in_), writes=_res(out))

    def scan(self, out, d0, d1, op0, op1, initial=0.0):
        o, a, b = out.ap, d0.ap, d1.ap
        self.S.op("dve", lambda e, it: e.tensor_tensor_scan(out=o, data0=a, data1=b, initial=initial, op0=op0, op1=op1),
                  reads=_res(d0, d1), writes=_res(out))

    def recip(self, out, in_):
        o, i = out.ap, in_.ap
        self.S.op("dve", lambda e, it: e.reciprocal(out=o, in_=i), reads=_res(in_), writes=_res(out))

    def memset(self, eng, out, val):
        o = out.ap
        self.S.op(eng, lambda e, it: e.memset(o, val), writes=_res(out))

    def dma(self, eng, out, in_, key):
        o, i = out.ap, in_.ap

        def fn(e, it):
            oo = o(it) if callable(o) else o
            ii = i(it) if callable(i) else i
            return e.dma_start(out=oo, in_=ii)
        self.S.dma(eng, fn, key, reads=_res(in_), writes=_res(out))

    def rstd(self, out, ssum, scale, eps, nhalf, tmp):
        self.act(tmp, ssum, AF.Ln, bias=self.const_col(eps), scale=scale)
        self.act(out, tmp, AF.Exp, scale=-0.5)

    def const_col(self, val):
        if val not in self.consts:
            t = self.sb([128, 1])
            self.memset("dve", t, float(val))
            self.consts[val] = t
        return self.consts[val]


CT = {}


def _ctab_layout():
    off = 0
    for name, w in (("ident", 128), ("bones", 128), ("swa_m", 768), ("ret_DT", 1024), ("ret_qdec", 512),
                    ("ret_kdec", 8), ("ret_gC", 4), ("rw_mNM", 256), ("rw_mL", 128), ("nhalf", 128)):
        CT[name] = (off, w)
        off += w
    return off


CT_W = _ctab_layout()


def make_ctab():
    t = np.zeros((128, CT_W), np.float64)

    def put(name, arr):
        o, w = CT[name]
        t[:, o:o + w] = np.asarray(arr, np.float64).reshape(128, w)

    idx = np.arange(128)
    put("ident", np.eye(128))
    bo = np.zeros((128, 128))
    bo[:64, :64] = 1
    bo[64:, 64:] = 1
    put("bones", bo)
    q = idx[:, None]
    j = idx[None, :]
    cur = np.where(j <= q, 0.0, NEG)
    prv = np.where(j > q, 0.0, NEG)
    allm = np.full((128, 128), NEG)
    put("swa_m", np.concatenate([cur, prv, prv, cur, cur, allm], axis=1))
    lg = np.log1p(-(2.0 ** (-5.0 - np.arange(8, dtype=np.float64))))
    rel = idx[None, :] - idx[:, None]
    DT = np.zeros((128, 8, 128))
    for h in range(8):
        DT[:, h, :] = np.where(rel >= 0, np.exp(np.maximum(rel, 0) * lg[h]), 0.0)
    put("ret_DT", DT)
    qd = np.zeros((128, 4, 128))
    gC = np.zeros((128, 4))
    for h in range(8):
        rows = slice((h % 2) * 64, (h % 2) * 64 + 64)
        qd[rows, h // 2, :] = np.exp((idx + 1.0) * lg[h])[None, :]
        gC[rows, h // 2] = np.exp(128 * lg[h])
    put("ret_qdec", qd)
    put("ret_gC", gC)
    put("ret_kdec", np.exp((127 - idx)[:, None] * lg[None, :]))
    r = idx[:, None]
    s = idx[None, :]
    put("rw_mNM", np.concatenate([(r < s) * 1.0, (r <= s) * 1.0], axis=1))
    put("rw_mL", (r > s) * 1.0)
    put("nhalf", np.full((128, 128), -0.5))
    return t.astype(np.float32)


def make_rope(T):
    half = HD // 2
    inv_freq = (10000.0 ** (-np.linspace(0.0, 1.0, half))).astype(np.float32)
    pos = np.arange(T, dtype=np.float32)
    ang = (pos[:, None] * inv_freq[None, :]).astype(np.float32).astype(np.float64)
    c = np.cos(ang)
    s = np.sin(ang)
    cc = np.concatenate([c, c], axis=1)
    ss = np.concatenate([-s, s], axis=1)
    return np.concatenate([cc, ss, cc * 0.125, ss * 0.125], axis=1).astype(np.float32)


class DT_:
    def __init__(self, ap2d):
        self.ap = ap2d
        self.ap4 = ap2d.rearrange("(n u p) d -> n u p d", u=2, p=128)

    def __call__(self, idx):
        return V(self.ap[idx * 128:(idx + 1) * 128, :])

    def quad(self):
        ap5 = self.ap.rearrange("(n v u p) d -> n v u p d", v=2, u=2, p=128)
        return lambda it: ap5[it].rearrange("v u p d -> p v u d")

    def pair(self, n0):
        ap4 = self.ap4
        if n0 == 0:
            return V(lambda it: ap4[it].rearrange("u p d -> p u d"))
        return V(lambda it: ap4[it + n0].rearrange("u p d -> p u d"))


class Prog:
    def __init__(self, T, plan):
        self.T = T
        self.plan = plan
        self.NT = T // 128

    def build(self):
        T = self.T
        nc = bass.Bass("TRN2", target_bir_lowering=False)
        self.nc = nc

        self.dins = {}
        self.x = self.din("x")
        self.ctab_d = self.din("ctab")
        self.rope_d = self.din("rope")
        self.y = nc.dram_tensor("y", [T, D], F32, kind="ExternalOutput").ap()
        self.hbuf = nc.dram_tensor("hbuf", [T, D], F32, kind="Internal").ap()

        with ExitStack() as st:
            c = Ctx(nc, st, 53000)
            self.c = c
            NT = self.NT
            self.x_t = DT_(self.x)
            self.h_t = DT_(self.hbuf)
            self.y_t = DT_(self.y)
            self.identf = self.ct_load("ident", "ct_ident")
            self.bones = self.ct_load("bones", "ct_bones")
            self.nhalf = self.ct_load("nhalf", "ct_nhalf")
            self.identb = c.sb([128, 128], BF16)
            c.copy("dve", self.identb, self.identf)
            for v_ in (RMS_EPS, RET_GN_EPS, RWKV_GN_EPS):
                c.const_col(v_)
            base = c.off
            src = self.x_t
            for kind, layer, last in self.plan:
                c.off = base
                c.S.new_segment()
                dst = self.y_t if last else self.h_t
                if kind == "B":
                    self.pass_ffn(layer, src, dst, last)
                elif kind == "O":
                    self.pass_swa(layer // 2, layer, src, dst)
                elif kind == "E":
                    self.pass_even(layer // 2, layer, src, dst)
                src = self.h_t
            c.S.emit(st)
        return nc

    SHAPES = {
        "norm1_g": [4, D], "norm2_g": [4, D], "final_g": [D], "even_w_in": [2, D, EVEN_IN], "even_w_out": [2, D, D],
        "rwkv_mu": [2, 1792], "rwkv_w0": [2, 512], "rwkv_w_up": [2, 64, 512], "rwkv_a0": [2, 512],
        "rwkv_a_up": [2, 64, 512], "rwkv_g_up": [2, 128, 512], "rwkv_k_k": [2, 512], "rwkv_k_a": [2, 512],
        "rwkv_r_k": [2, 512], "rwkv_ln_g": [2, 512], "rwkv_ln_b": [2, 512], "swa_w_qkv": [2, D, SWA_QKV],
        "swa_b_qkv": [2, SWA_QKV], "swa_sinks": [2, 16], "swa_w_o": [2, D, D], "swa_b_o": [2, D],
        "ffn_w_gate": [4, D, DFF], "ffn_w_up": [4, D, DFF], "ffn_w_down": [4, DFF, D], "ctab": [128, CT_W],
    }

    def din(self, name):
        if name not in self.dins:
            if name == "x":
                shape = [self.T, D]
            elif name == "rope":
                shape = [self.T, 256]
            else:
                shape = self.SHAPES[name]
            self.dins[name] = self.nc.dram_tensor(name, list(shape), F32, kind="ExternalInput").ap()
        return self.dins[name]

    @staticmethod
    def dtile(ap2d, idx):
        if isinstance(idx, int):
            return V(ap2d[idx * 128:(idx + 1) * 128, :])
        m, a = idx
        assert m == 2
        ap4 = ap2d.rearrange("(n u p) d -> n u p d", u=2, p=128)
        n0, u = a // 2, a % 2
        if n0 == 0:
            return V(lambda it: ap4[it, u, :, :])
        return V(lambda it: ap4[it + n0, u, :, :])

    def ct_load(self, name, key):
        o, w = CT[name]
        t = self.c.sb([128, w])
        self.c.dma("sp", t, V(self.ctab_d[:, o:o + w]), key)
        return t

    def load_w(self, dst, src_ap, key, kt, eng="pool"):
        c = self.c
        for k in range(kt):
            c.dma(eng, dst[:, k, :], V(src_ap[k * 128:(k + 1) * 128, :], Res()), key)

    def bcast_load(self, vec_ap, n, key):
        c = self.c
        t = c.sb([128, n])
        c.dma("sp", t, V(vec_ap.partition_broadcast(128)), key)
        return t

    def col_load(self, vec_ap, nt, key):
        c = self.c
        t = c.sb([128, nt])
        c.S.dma("sp", (lambda o, i: (lambda e, it: e.dma_start(out=o, in_=i, allow_slow_non_contiguous=True)))(
            t.ap, vec_ap.rearrange("(t p) -> p t", p=128)), key, writes=[t.res])
        return t

    def rmsnorm_to_bf16(self, hb, gb, nb, junk, ss, tmp, rs):
        c = self.c
        c.act(junk, hb, AF.Square, accum=ss)
        c.rstd(rs, ss, 1.0 / D, RMS_EPS, self.nhalf[:, 0:1], tmp)
        c.stt(nb, hb, rs, gb, ALU.mult, ALU.mult)

    def transpose_bf16(self, dst3, src, nblk, evac="act"):
        c = self.c
        for g0 in range(0, nblk, 8):
            n = min(8, nblk - g0)
            bk = c.bank().bitcast(BF16)
            for j in range(n):
                c.tr(bk[:, j * 128:(j + 1) * 128], src[:, (g0 + j) * 128:(g0 + j + 1) * 128], self.identb)
            c.copy(evac, dst3[:, g0:g0 + n, :], bk[:, 0:n * 128].re("p (a b) -> p a b", b=128))

    def pass_ffn(self, layer, src, dst, last):
        c = self.c
        TB = 2
        ntile = self.NT // TB
        wg = c.sb([128, 8, DFF], BF16)
        wu = c.sb([128, 8, DFF], BF16)
        wd = c.sb([128, 22, D], BF16)
        self.load_w(wg, self.din("ffn_w_gate")[layer], "wg", 8)
        self.load_w(wu, self.din("ffn_w_up")[layer], "wu", 8)
        self.load_w(wd, self.din("ffn_w_down")[layer], "wd", 22)
        g2 = self.bcast_load(self.din("norm2_g")[layer], D, "g2")
        gf = self.bcast_load(self.din("final_g"), D, "gf") if last else None
        NBUF = 2
        hb4 = c.sb([128, NBUF, TB, D])
        hbp = [hb4.sub((slice(None), s_)) for s_ in range(NBUF)]
        hb = [[hbp[s_][:, u_, :] for u_ in range(TB)] for s_ in range(NBUF)]
        nb = [[c.sb([128, D], BF16) for _ in range(TB)] for _ in range(NBUF)]
        nT = [[c.sb([128, 8, 128], BF16) for _ in range(TB)] for _ in range(NBUF)]
        aT = [c.sb([128, 128 * TB], BF16) for _ in range(22)]
        junk = c.sb([128, D], BF16)
        sil = [c.sb([128, 128 * TB]) for _ in range(3)]
        st = [c.sb([128, 8]) for _ in range(NBUF)]

        def stage_load():
            o_, i_ = hb4.ap, src.quad()
            c.S.dma("sp", lambda e, it: e.dma_start(out=o_, in_=i_(it)), "ffn_h", writes=[hbp[0].res, hbp[1].res])

        def stage_store():
            i_, o_ = hb4.ap, dst.quad()
            c.S.dma("act", lambda e, it: e.dma_start(out=o_(it), in_=i_), "ffn_o", reads=[hbp[0].res, hbp[1].res])

        def stage_norm(i, s):
            for u in range(TB):
                self.rmsnorm_to_bf16(hb[s][u], g2, nb[s][u], junk, st[s][:, u:u + 1], st[s][:, 2 + u:3 + u],
                                     st[s][:, 4 + u:5 + u])
                self.transpose_bf16(nT[s][u], nb[s][u], 8)

        def stage_main(i, s):
            for f in range(22):
                bk = c.bank()
                for half, w in ((0, wg), (1, wu)):
                    for u in range(TB):
                        for k in range(8):
                            c.mm(bk[:, half * 256 + u * 128: half * 256 + (u + 1) * 128],
                                 w[:, k, f * 128:(f + 1) * 128], nT[s][u][:, k, :], start=(k == 0), stop=(k == 7))
                sl = sil[f % 3]
                c.act(sl, bk[:, 0:128 * TB], AF.Silu)
                c.tt("dve", aT[f], sl, bk[:, 256:256 + 128 * TB], ALU.mult)
            for u in range(TB):
                for half in range(2):
                    bk = c.bank()
                    for f in range(22):
                        c.mm(bk, aT[f][:, u * 128:(u + 1) * 128], wd[:, f, half * 512:(half + 1) * 512],
                             start=(f == 0), stop=(f == 21))
                    hs = hb[s][u][:, half * 512:(half + 1) * 512]
                    c.tt("dve", hs, bk, hs, ALU.add)
                if last:
                    self.final_norm(hb[s][u], gf, hb[s][u], junk, st[s][:, 6:7], st[s][:, 7:8], st[s][:, 5:6])

        c.S.new_segment(loop=(0, ntile // 2))
        stage_load()
        stage_norm("L", 0)
        stage_norm("L", 1)
        stage_main("L", 0)
        stage_main("L", 1)
        stage_store()
        c.S.new_segment()

    def final_norm(self, hb, gf, ob, junk, ss, tmp, rs):
        c = self.c
        c.act(junk, hb, AF.Square, accum=ss)
        c.rstd(rs, ss, 1.0 / D, RMS_EPS, self.nhalf[:, 0:1], tmp)
        c.stt(ob, hb, rs, gf, ALU.mult, ALU.mult)

    def pass_swa(self, li, layer, src, dst):
        c = self.c
        NT = self.NT
        wq = c.sb([128, 8, 1024], BF16)
        wk = c.sb([128, 8, 4, 128], BF16)
        wv = c.sb([128, 8, 256], BF16)
        wo = c.sb([128, 8, 1024], BF16)
        Wd = self.din("swa_w_qkv")[li]
        self.load_w(wq, Wd[:, 0:1024], "wq", 8)
        for k in range(8):
            for dup in range(2):
                c.dma("pool", wk[:, k, :, dup * 64:(dup + 1) * 64],
                      V(Wd[k * 128:(k + 1) * 128, 1024:1280].rearrange("p (a b) -> p a b", b=64)), "wk")
        self.load_w(wv, Wd[:, 1280:1536], "wv", 8)
        self.load_w(wo, self.din("swa_w_o")[li], "wo", 8)
        g1 = self.bcast_load(self.din("norm1_g")[layer], D, "g1")
        bq = self.col_load(self.din("swa_b_qkv")[li, 0:1024], 8, "bq")
        bk = c.sb([128, 4])
        for dup in range(2):
            c.S.dma("sp", (lambda o, i: (lambda e, it: e.dma_start(out=o, in_=i, allow_slow_non_contiguous=True)))(
                bk.ap[dup * 64:(dup + 1) * 64, :], self.din("swa_b_qkv")[li, 1024:1280].rearrange("(a p) -> p a", p=64)),
                "bk", writes=[bk.res])
        bv = self.bcast_load(self.din("swa_b_qkv")[li, 1280:1536], 256, "bv")
        bo = self.bcast_load(self.din("swa_b_o")[li], D, "bo")
        snk = self.bcast_load(self.din("swa_sinks")[li], 16, "snk")
        mtab = self.ct_load("swa_m", "ct_swa")
        masks = [mtab[:, 256 * m: 256 * (m + 1)] for m in range(3)]

        NBUF = 2
        hbp = c.sb([128, 2, D])
        hb = [hbp[:, s_, :] for s_ in range(NBUF)]
        nb = [c.sb([128, D], BF16) for _ in range(NBUF)]
        nT = [c.sb([128, 8, 128], BF16) for _ in range(NBUF)]
        qT = [c.sb([128, 8, 128], BF16) for _ in range(NBUF)]
        kT = c.sb([128, 4, 256], BF16)
        kTh = [kT.sub((slice(None), slice(None), slice(h * 128, (h + 1) * 128))) for h in range(2)]
        vb = [c.sb([128, 256], BF16) for _ in range(2)]
        junk = c.sb([128, D], BF16)
        st = [c.sb([128, 8]) for _ in range(NBUF)]
        sm = [c.sb([128, 4, 256]) for _ in range(2)]
        pb = [c.sb([128, 4, 256], BF16) for _ in range(2)]
        pT = [c.sb([128, 8, 128], BF16) for _ in range(4)]
        mx = [c.sb([128, 16]) for _ in range(NBUF)]
        nm = [c.sb([128, 16]) for _ in range(NBUF)]
        rsum = [c.sb([128, 16]) for _ in range(NBUF)]
        es = [c.sb([128, 16]) for _ in range(NBUF)]
        rec = [c.sb([128, 16]) for _ in range(NBUF)]
        ob = [c.sb([128, D], BF16) for _ in range(NBUF)]
        oT = [c.sb([128, 8, 128], BF16) for _ in range(NBUF)]
        c.memset("dve", kT, 0.0)
        for v_ in vb:
            c.memset("dve", v_, 0.0)

        def stage_pre(i, s):
            if isinstance(i, int):
                c.dma("sp", hb[s], src(i), "swa_h%d" % s)
            elif s == 0:
                c.dma("sp", hbp, src.pair(1), "swa_hp")
            self.rmsnorm_to_bf16(hb[s], g1, nb[s], junk, st[s][:, 0:1], st[s][:, 1:2], st[s][:, 2:3])
            self.transpose_bf16(nT[s], nb[s], 8)
            c.tt("dve", hb[s], hb[s], bo, ALU.add)

        def stage_main(i, s, first=False):
            blk = s
            for g in range(2):
                bkq = c.bank()
                for j in range(4):
                    ft = g * 4 + j
                    for k in range(8):
                        c.mm(bkq[:, j * 128:(j + 1) * 128], wq[:, k, ft * 128:(ft + 1) * 128], nT[s][:, k, :],
                             start=(k == 0), stop=(k == 7))
                for j in range(4):
                    ft = g * 4 + j
                    c.act(qT[s][:, ft, :], bkq[:, j * 128:(j + 1) * 128], AF.Identity, bias=bq[:, ft:ft + 1])
            bkk = c.bank()
            for kv in range(4):
                for k in range(8):
                    c.mm(bkk[:, kv * 128:(kv + 1) * 128], wk[:, k, kv, :], nT[s][:, k, :], start=(k == 0), stop=(k == 7))
            for kv in range(4):
                c.ts("dve", kTh[blk][:, kv, :], bkk[:, kv * 128:(kv + 1) * 128], bk[:, kv:kv + 1], None, ALU.add)
            bkv = c.bank()
            for k in range(8):
                c.mm(bkv[:, 0:256], nT[s][:, k, :], wv[:, k, :], start=(k == 0), stop=(k == 7))
            c.tt("dve", vb[blk], bkv[:, 0:256], bv, ALU.add)
            mask = masks[2] if first else masks[blk]
            for g in range(4):
                bks2 = [c.bank(), c.bank()]
                for j in range(4):
                    hq = g * 4 + j
                    r0 = (hq % 2) * 64
                    c.mm(bks2[hq % 2][:, (j // 2) * 256:(j // 2 + 1) * 256], qT[s][r0:r0 + 64, hq // 2, :],
                         V(kT.ap[r0:r0 + 64, g, :], kTh[0].res), start=True, stop=True, extra_r=[kTh[1].res])
                for j in range(4):
                    hq = g * 4 + j
                    c.stt(sm[g % 2][:, j, :], bks2[hq % 2][:, (j // 2) * 256:(j // 2 + 1) * 256], 0.125, mask,
                          ALU.mult, ALU.add)
                c.red(mx[s][:, g * 4:(g + 1) * 4], sm[g % 2], ALU.max)
                c.tt("dve", mx[s][:, g * 4:(g + 1) * 4], mx[s][:, g * 4:(g + 1) * 4], snk[:, g * 4:(g + 1) * 4], ALU.max)
                c.ts("dve", nm[s][:, g * 4:(g + 1) * 4], mx[s][:, g * 4:(g + 1) * 4], -1.0, None, ALU.mult)
                for j in range(4):
                    hq = g * 4 + j
                    c.act(pb[g % 2][:, j, :], sm[g % 2][:, j, :], AF.Exp, bias=nm[s][:, hq:hq + 1],
                          accum=rsum[s][:, hq:hq + 1])
                bkt = c.bank().bitcast(BF16)
                for j in range(4):
                    for half in range(2):
                        c.tr(bkt[:, (j * 2 + half) * 128:(j * 2 + half + 1) * 128],
                             pb[g % 2][:, j, half * 128:(half + 1) * 128], self.identb)
                c.copy("act", pT[g], bkt.re("p (a b) -> p a b", b=128))
            c.tt("dve", es[s], snk, mx[s], ALU.subtract)
            c.act(es[s], es[s], AF.Exp)
            c.tt("dve", es[s], es[s], rsum[s], ALU.add)
            c.recip(rec[s], es[s])
            for g2 in range(2):
                bko = c.bank()
                for j in range(8):
                    hq = g2 * 8 + j
                    g = hq // 4
                    for half in range(2):
                        vsrc = vb[half]
                        c.mm(bko[:, j * 64:(j + 1) * 64], pT[g][:, (hq % 4) * 2 + half, :], vsrc[:, g * 64:(g + 1) * 64],
                             start=(half == 0), stop=(half == 1))
                c.tt("dve", ob[s][:, g2 * 512:(g2 + 1) * 512].re("p (a b) -> p a b", b=64),
                     bko.re("p (a b) -> p a b", b=64),
                     rec[s][:, g2 * 8:(g2 + 1) * 8].un(2).bc([128, 8, 64]), ALU.mult)
            self.transpose_bf16(oT[s], ob[s], 8)
            for half in range(2):
                bkp = c.bank()
                for k in range(8):
                    c.mm(bkp, oT[s][:, k, :], wo[:, k, half * 512:(half + 1) * 512], start=(k == 0), stop=(k == 7))
                hs = hb[s][:, half * 512:(half + 1) * 512]
                c.tt("dve", hs, bkp, hs, ALU.add)
            if isinstance(i, int):
                c.dma("act", dst(i), hb[s], "swa_o%d" % s)
            elif s == 1:
                c.dma("act", dst.pair(1), hbp, "swa_op")

        stage_pre(0, 0)
        stage_pre(1, 1)
        stage_main(0, 0, first=True)
        stage_main(1, 1)
        if NT > 2:
            c.S.new_segment(loop=(0, NT // 2 - 1))
            stage_pre("L", 0)
            stage_pre("L", 1)
            stage_main("L", 0)
            stage_main("L", 1)
            c.S.new_segment()

    def pass_even(self, li, layer, src, dst):
        c = self.c
        NT = self.NT
        f3 = lambda v, b=128: v.re("p (a b) -> p a b", b=b)
        w_in = c.sb([128, 8, EVEN_IN], BF16)
        w_out = c.sb([128, 8, D], BF16)
        self.load_w(w_in, self.din("even_w_in")[li], "w_in", 8)
        self.load_w(w_out, self.din("even_w_out")[li], "w_out", 8)
        W1 = c.sb([128, 512], BF16)
        W2 = c.sb([128, 512], BF16)
        GU = c.sb([128, 512], BF16)
        c.memset("dve", W1, 0.0)
        c.memset("dve", W2, 0.0)
        c.dma("pool", W1[0:64, :], V(self.din("rwkv_w_up")[li]), "W1")
        c.dma("pool", W2[64:128, :], V(self.din("rwkv_a_up")[li]), "W2")
        c.dma("pool", GU, V(self.din("rwkv_g_up")[li]), "GU")
        g1 = self.bcast_load(self.din("norm1_g")[layer], D, "g1")
        mu = self.col_load(self.din("rwkv_mu")[li], 14, "mu")
        w0 = self.col_load(self.din("rwkv_w0")[li], 4, "w0")
        a0 = self.col_load(self.din("rwkv_a0")[li], 4, "a0")
        k_k = self.col_load(self.din("rwkv_k_k")[li], 4, "k_k")
        k_a = self.col_load(self.din("rwkv_k_a")[li], 4, "k_a")
        r_k = self.col_load(self.din("rwkv_r_k")[li], 4, "r_k")
        ln_g = self.col_load(self.din("rwkv_ln_g")[li], 4, "ln_g")
        ln_b = self.col_load(self.din("rwkv_ln_b")[li], 4, "ln_b")
        omu = c.sb([128, 14])
        c.ts("dve", omu, mu, -1.0, 1.0, ALU.mult, ALU.add)
        omka = c.sb([128, 4])
        c.ts("dve", omka, k_a, -1.0, 1.0, ALU.mult, ALU.add)
        c.ts("dve", w0, w0, -1.0, None, ALU.mult)
        c.ts("dve", a0, a0, -1.0, None, ALU.mult)
        DT = self.ct_load("ret_DT", "ct_DT")
        qdec = self.ct_load("ret_qdec", "ct_qdec")
        kdec = self.ct_load("ret_kdec", "ct_kdec")
        gC = self.ct_load("ret_gC", "ct_gC")
        mNM = self.ct_load("rw_mNM", "ct_mNM")
        mL = self.ct_load("rw_mL", "ct_mL")
        zeros = c.sb([128, 128])
        c.memset("dve", zeros, 0.0)
        nh = self.nhalf
        CW = -math.exp(-0.5)

        NBUF = 2
        hbp = c.sb([128, 2, D])
        hb = [hbp[:, s_, :] for s_ in range(NBUF)]
        rpp = c.sb([128, 2, 256])
        rp = [rpp[:, s_, :] for s_ in range(NBUF)]
        rope_t = DT_(self.rope_d)
        nb = c.sb([128, D], BF16)
        nT = [c.sb([128, 8, 130], BF16) for _ in range(NBUF)]
        mixT = [c.sb([128, 8, 128], BF16) for _ in range(NBUF)]
        st = [c.sb([128, 8]) for _ in range(NBUF)]
        fa = c.sb([128, 512]); fb = c.sb([128, 512]); th = c.sb([128, 512]); sg2 = c.sb([128, 512])
        qr = c.sb([128, 512], BF16); kr = c.sb([128, 512], BF16); kd = c.sb([128, 512], BF16); vb = c.sb([128, 512], BF16)
        qTp = c.sb([128, 2, 4, 128], BF16); kTr = c.sb([128, 4, 128], BF16); qdT = c.sb([128, 4, 128], BF16)
        PT = c.sb([128, 8, 128], BF16)
        S32 = c.sb([128, 4, 64]); Spad = c.sb([128, 2, 4, 64], BF16)
        ro = c.sb([128, 512], BF16)
        gst = c.sb([128, 64])
        c.memset("dve", qTp, 0.0); c.memset("dve", S32, 0.0); c.memset("dve", Spad, 0.0)
        B = [c.sb([128, 512]) for _ in range(16)]
        AR = c.sb([128, 4, 2, 128])
        Btp = c.sb([128, 2, 4, 128]); Ktp = c.sb([128, 2, 4, 128])
        wa = c.sb([128, 128]); gl = c.sb([128, 128]); tg = c.sb([128, 128])
        wab = c.sb([128, 128], BF16); sglb = c.sb([128, 128], BF16)
        Stp = c.sb([128, 2, 4, 64])
        gst2 = c.sb([128, 64])
        c.memset("dve", Btp, 0.0); c.memset("dve", Ktp, 0.0); c.memset("dve", Stp, 0.0)
        HS = 4
        hNM1 = [c.sb([128, 256]) for _ in range(HS)]
        hNM2 = [c.sb([128, 256]) for _ in range(HS)]
        hPTa = [c.sb([128, 128]) for _ in range(HS)]
        hPb = [c.sb([128, 128]) for _ in range(HS)]
        hPTb = [c.sb([128, 128]) for _ in range(HS)]
        hQ = [[c.sb([128, 128]) for _ in range(2)] for _ in range(HS)]
        hX = [c.sb([128, 64]) for _ in range(HS)]
        ybank = c.banks[7]

        def gnorm(bank, osq, t1, gs, eps, stage=False):
            if stage:
                c.copy("dve", t1, bank)
                bank = t1
            b3 = f3(bank, 64)
            c.red(gs[:, 0:8], b3, ALU.add)
            c.act(osq, bank, AF.Square)
            c.red(gs[:, 8:16], f3(osq, 64), ALU.add)
            c.ts("dve", gs[:, 16:24], gs[:, 0:8], 1.0 / 64, None, ALU.mult)
            c.tt("dve", gs[:, 24:32], gs[:, 16:24], gs[:, 16:24], ALU.mult)
            c.stt(gs[:, 32:40], gs[:, 8:16], 1.0 / 64, gs[:, 24:32], ALU.mult, ALU.subtract)
            c.rstd(gs[:, 40:48], gs[:, 32:40], 1.0, eps, nh[:, 0:8], gs[:, 48:56])
            c.tt("dve", f3(t1, 64), b3, gs[:, 16:24].un(2).bc([128, 8, 64]), ALU.subtract)
            c.tt("dve", f3(t1, 64), f3(t1, 64), gs[:, 40:48].un(2).bc([128, 8, 64]), ALU.mult)

        def stage_pre(i, s, first=False):
            if isinstance(i, int):
                c.dma("sp", hb[s], src(i), "ev_h%d" % s)
                c.dma("sp", rp[s], rope_t(i), "ev_rp%d" % s)
            elif s == 0:
                c.dma("sp", hbp, src.pair(1), "ev_hp")
                c.dma("sp", rpp, rope_t.pair(1), "ev_rpp")
            self.rmsnorm_to_bf16(hb[s], g1, nb, nb, st[s][:, 0:1], st[s][:, 1:2], st[s][:, 2:3])
            self.transpose_bf16(nT[s][:, :, 2:130], nb, 8)
            if first:
                c.memset("dve", nT[s][:, :, 0:2], 0.0)
            else:
                c.copy("dve", nT[s][:, :, 1:2], nT[1 - s][:, :, 129:130])

        def stage_ret(i, s):
            r = rp[s]
            for cg in range(4):
                bk = c.bank()
                for k in range(8):
                    c.mm(bk, nT[s][:, k, 2:130], w_in[:, k, cg * 512:(cg + 1) * 512], start=(k == 0), stop=(k == 7))
                b3 = f3(bk, 64)
                if cg < 2:
                    ro_ = cg * 128
                    dst_ = qr if cg == 0 else kr
                    c.tt("dve", f3(fa, 64), b3, r[:, ro_:ro_ + 64].un(1).bc([128, 8, 64]), ALU.mult)
                    c.tt("dve", f3(fb, 64)[:, :, 0:32], b3[:, :, 32:64], r[:, ro_ + 64:ro_ + 96].un(1).bc([128, 8, 32]), ALU.mult)
                    c.tt("dve", f3(fb, 64)[:, :, 32:64], b3[:, :, 0:32], r[:, ro_ + 96:ro_ + 128].un(1).bc([128, 8, 32]), ALU.mult)
                    c.tt("dve", dst_, fa, fb, ALU.add)
                    if cg == 1:
                        c.tt("dve", f3(kd, 64), f3(kr, 64), kdec.un(2).bc([128, 8, 64]), ALU.mult)
                elif cg == 2:
                    c.copy("act", vb, bk)
                else:
                    c.act(th, bk, AF.Exp, scale=-1.0)
                    c.ts("dve", th, th, 1.0, None, ALU.add)
                    c.recip(th, th)
                    c.tt("dve", sg2, bk, th, ALU.mult)
            bq_ = c.bank().bitcast(BF16)
            for p in range(4):
                c.tr(bq_[:, p * 128:(p + 1) * 128], qr[:, p * 128:(p + 1) * 128], self.identb)
                c.tr(bq_[:, 512 + p * 128:512 + (p + 1) * 128], kr[:, p * 128:(p + 1) * 128], self.identb)
            bq3 = f3(bq_[:, 0:512])
            c.tt("dve", qdT, bq3, f3(qdec), ALU.mult)
            c.copy("act", qTp[0:64, 0, :, :], bq3[0:64, :, :])
            c.copy("act", qTp[64:128, 1, :, :], bq3[64:128, :, :])
            c.copy("act", kTr, f3(bq_[:, 512:1024]))
            for g in range(2):
                bs = c.bank()
                for j in range(4):
                    h = g * 4 + j
                    c.mm(bs[:, j * 128:(j + 1) * 128], kTr[:, h // 2, :], qTp[:, h % 2, h // 2, :])
                c.tt("dve", PT[:, g * 4:(g + 1) * 4, :], f3(bs), f3(DT)[:, g * 4:(g + 1) * 4, :], ALU.mult)
            bo_ = c.bank()
            for h in range(8):
                c.mm(bo_[:, h * 64:(h + 1) * 64], PT[:, h, :], vb[:, h * 64:(h + 1) * 64], start=True, stop=False)
                c.mm(bo_[:, h * 64:(h + 1) * 64], qdT[:, h // 2, :], Spad[:, h % 2, h // 2, :], start=False, stop=True)
            bS = c.bank()
            for p in range(4):
                c.mm(bS[:, p * 128:(p + 1) * 128], kd[:, p * 128:(p + 1) * 128], vb[:, p * 128:(p + 1) * 128])
            bS3 = f3(bS)
            c.tt("dve", S32, S32, gC.un(2).bc([128, 4, 64]), ALU.mult)
            c.tt("dve", S32[0:64], S32[0:64], bS3[0:64, :, 0:64], ALU.add)
            c.tt("dve", S32[64:128], S32[64:128], bS3[64:128, :, 64:128], ALU.add)
            c.copy("act", Spad[0:64, 0], S32[0:64])
            c.copy("act", Spad[64:128, 1], S32[64:128])
            gnorm(bo_, fa, fb, gst, RET_GN_EPS)
            c.tt("dve", ro, fb, sg2, ALU.mult)
            self.transpose_bf16(mixT[s][:, 0:4, :], ro, 4)

        def stage_rwkv(i, s):
            rT, kraw, vT, lw, al, gT, L, E1, E2, E3, E4, kkf, tmp, inv, kp = B[1:16]
            dests = [rT, kraw, vT]
            for g0 in range(0, 14, 3):
                bk = c.bank()
                fts = list(range(g0, min(g0 + 3, 14)))
                for j, ft in enumerate(fts):
                    for k in range(8):
                        c.mm(bk[:, j * 129:(j + 1) * 129], w_in[:, k, 2048 + ft * 128:2048 + (ft + 1) * 128],
                             nT[s][:, k, 1:130], start=(k == 0), stop=(k == 7))
                for j, ft in enumerate(fts):
                    if ft < 12:
                        d_ = f3(dests[ft // 4])[:, ft % 4, :]
                    elif ft == 12:
                        d_ = wa
                    else:
                        d_ = gl
                    c.act(tg, bk[:, j * 129:j * 129 + 128], AF.Identity, scale=mu[:, ft:ft + 1])
                    c.stt(d_, bk[:, j * 129 + 1:j * 129 + 129], omu[:, ft:ft + 1], tg, ALU.mult, ALU.add)
            c.act(tg[0:64, :], wa[0:64, :], AF.Exp, scale=-2.0)
            c.ts("dve", tg[0:64, :], tg[0:64, :], 1.0, None, ALU.add)
            c.recip(tg[0:64, :], tg[0:64, :])
            c.ts("dve", wab[0:64, :], tg[0:64, :], 2.0, -1.0, ALU.mult, ALU.add)
            c.copy("act", wab[64:128, :], wa[64:128, :])
            c.act(tg, gl, AF.Exp, scale=-1.0)
            c.ts("dve", tg, tg, 1.0, None, ALU.add)
            c.recip(tg, tg)
            c.copy("act", sglb, tg)
            bw = c.bank(); ba = c.bank(); bg = c.bank()
            for ft in range(4):
                c.mm(bw[:, ft * 128:(ft + 1) * 128], W1[:, ft * 128:(ft + 1) * 128], wab)
            for ft in range(4):
                c.mm(ba[:, ft * 128:(ft + 1) * 128], W2[:, ft * 128:(ft + 1) * 128], wab)
            for ft in range(4):
                c.mm(bg[:, ft * 128:(ft + 1) * 128], GU[:, ft * 128:(ft + 1) * 128], sglb)
            for ft in range(4):
                c.act(f3(lw)[:, ft, :], bw[:, ft * 128:(ft + 1) * 128], AF.Exp, bias=w0[:, ft:ft + 1], scale=-1.0)
                c.act(f3(al)[:, ft, :], ba[:, ft * 128:(ft + 1) * 128], AF.Exp, bias=a0[:, ft:ft + 1], scale=-1.0)
            c.ts("dve", lw, lw, 1.0, None, ALU.add)
            c.recip(lw, lw)
            c.ts("dve", lw, lw, CW, None, ALU.mult)
            c.ts("dve", al, al, 1.0, None, ALU.add)
            c.recip(al, al)
            c.copy("act", gT, bg)
            for ft in range(4):
                c.scan(f3(L)[:, ft, :], f3(lw)[:, ft, :], zeros, ALU.add, ALU.add)
            c.act(E1, L, AF.Exp)
            c.act(E2, L, AF.Exp, scale=-1.0)
            c.tt("dve", E3, L, lw, ALU.subtract)
            c.act(E3, E3, AF.Exp)
            for ft in range(4):
                c.act(f3(E4)[:, ft, :], f3(L)[:, ft, :], AF.Exp, bias=f3(L)[:, ft, 127:128], scale=-1.0)
            c.tt("dve", f3(kkf), f3(kraw), k_k.un(2).bc([128, 4, 128]), ALU.mult)
            c.act(tmp, kkf, AF.Square)
            bss = c.bank()
            c.mm(bss, self.bones, tmp)
            c.ts("dve", tmp, bss, 1e-19, None, ALU.max)
            c.act(tmp, tmp, AF.Ln)
            c.act(inv, tmp, AF.Exp, scale=-0.5)
            c.tt("dve", kkf, kkf, inv, ALU.mult)
            kk = kkf
            c.tt("dve", f3(tmp), f3(al), k_a.un(2).bc([128, 4, 128]), ALU.mult)
            c.tt("dve", f3(tmp), f3(tmp), omka.un(2).bc([128, 4, 128]), ALU.add)
            c.tt("dve", kp, kraw, tmp, ALU.mult)
            b_ = inv
            c.tt("dve", b_, kk, al, ALU.mult)
            c.stt(AR[:, :, 0, :], f3(kk), -1.0, f3(E3), ALU.mult, ALU.mult)
            c.tt("dve", AR[:, :, 1, :], f3(rT), f3(E1), ALU.mult)
            c.tt("dve", Btp[0:64, 0], f3(b_)[0:64], f3(E2)[0:64], ALU.mult)
            c.tt("dve", Btp[64:128, 1], f3(b_)[64:128], f3(E2)[64:128], ALU.mult)
            c.tt("dve", Ktp[0:64, 0], f3(kp)[0:64], f3(E2)[0:64], ALU.mult)
            c.tt("dve", Ktp[64:128, 1], f3(kp)[64:128], f3(E2)[64:128], ALU.mult)
            Bp = E3
            Kp = E2
            c.tt("dve", Bp, b_, E4, ALU.mult)
            c.tt("dve", Kp, kp, E4, ALU.mult)
            c.tt("dve", tmp, rT, kp, ALU.mult)
            c.tt("dve", f3(tmp), f3(tmp), r_k.un(2).bc([128, 4, 128]), ALU.mult)
            bbn = c.bank()
            c.mm(bbn, self.bones, tmp)
            bonus = lw
            c.tt("dve", bonus, bbn, vT, ALU.mult)
            Bp_tok, Kp_tok, V_tok = L, E4, kkf
            for src_, dst_ in ((Bp, Bp_tok), (Kp, Kp_tok), (vT, V_tok)):
                bt_ = c.bank()
                for ft in range(4):
                    c.tr(bt_[:, ft * 128:(ft + 1) * 128], f3(src_)[:, ft, :], self.identf)
                c.copy("act", dst_, bt_)
            Us = inv
            for wave in range(2):
                heads = list(range(wave * 4, wave * 4 + 4))
                for q_, h in enumerate(heads):
                    p = h // 2
                    par = h % 2
                    b1 = c.bank()
                    c.mm(b1[:, 0:256], Btp[:, par, p, :], AR[:, p, :, :].re("p a b -> p (a b)"))
                    c.mm(b1[:, 256:512], Ktp[:, par, p, :], AR[:, p, :, :].re("p a b -> p (a b)"))
                    b2 = c.bank()
                    c.mm(b2[:, 0:128], AR[:, p, 0, :], Btp[:, par, p, :])
                    c.tt("dve", hNM1[q_], b1[:, 0:256], mNM, ALU.mult)
                    c.tt("dve", hNM2[q_], b1[:, 256:512], mNM, ALU.mult)
                    c.tt("dve", hPTa[q_], b2[:, 0:128], mL, ALU.mult)
                    c.tt("dve", hQ[q_][0], hNM1[q_][:, 0:128], self.identf, ALU.add)
                cur = [(hNM1[q_][:, 0:128], hPTa[q_]) for q_ in range(4)]
                nxt = [(hPb[q_], hPTb[q_]) for q_ in range(4)]
                for lev in range(1, 7):
                    bks = []
                    for q_ in range(4):
                        P_, PT_ = cur[q_]
                        bk = c.bank()
                        c.mm(bk[:, 128:256], P_, PT_)
                        if lev < 6:
                            c.mm(bk[:, 0:128], PT_, P_)
                        bks.append(bk)
                    for q_ in range(4):
                        Pn, PTn = nxt[q_]
                        c.copy("dve", PTn, bks[q_][:, 128:256])
                        if lev < 6:
                            c.copy("dve", Pn, bks[q_][:, 0:128])
                    bks2 = []
                    for q_ in range(4):
                        Pn, PTn = nxt[q_]
                        bk = c.bank()
                        c.mm(bk[:, 0:128], PTn, hQ[q_][(lev - 1) % 2])
                        bks2.append(bk)
                    for q_ in range(4):
                        c.tt("dve", hQ[q_][lev % 2], bks2[q_][:, 0:128], hQ[q_][(lev - 1) % 2], ALU.add)
                    cur, nxt = nxt, cur
                for q_, h in enumerate(heads):
                    p = h // 2
                    par = h % 2
                    Qf = hQ[q_][0]
                    bx = c.bank()
                    c.mm(bx[:, 0:64], AR[:, p, 0, :], Stp[:, par, p, :], start=True, stop=False)
                    c.mm(bx[:, 0:64], hNM2[q_][:, 0:128], V_tok[:, h * 64:(h + 1) * 64], start=False, stop=True)
                    c.copy("dve", hX[q_], bx[:, 0:64])
                    c.mm(bx[:, 64:128], Qf, hX[q_])
                    c.copy("dve", Us[:, h * 64:(h + 1) * 64], bx[:, 64:128])
                    yb = ybank[:, h * 64:(h + 1) * 64]
                    c.mm(yb, AR[:, p, 1, :], Stp[:, par, p, :], start=True, stop=False)
                    c.mm(yb, hNM1[q_][:, 128:256], Us[:, h * 64:(h + 1) * 64], start=False, stop=False)
                    c.mm(yb, hNM2[q_][:, 128:256], V_tok[:, h * 64:(h + 1) * 64], start=False, stop=True)
                    if par == 1:
                        bsu = c.bank()
                        c.mm(bsu[:, 0:128], Bp_tok[:, p * 128:(p + 1) * 128], Us[:, p * 128:(p + 1) * 128], start=True, stop=False)
                        c.mm(bsu[:, 0:128], Kp_tok[:, p * 128:(p + 1) * 128], V_tok[:, p * 128:(p + 1) * 128], start=False, stop=True)
                        c.stt(Stp[0:64, 0, p, :], Stp[0:64, 0, p, :], f3(E1)[0:64, p, 127:128], bsu[0:64, 0:64], ALU.mult, ALU.add)
                        c.stt(Stp[64:128, 1, p, :], Stp[64:128, 1, p, :], f3(E1)[64:128, p, 127:128], bsu[64:128, 64:128], ALU.mult, ALU.add)
            yn, osq = tmp, kp
            gnorm(ybank, osq, yn, gst2, RWKV_GN_EPS, stage=True)
            bt_ = c.bank()
            for ft in range(4):
                c.tr(bt_[:, ft * 128:(ft + 1) * 128], yn[:, ft * 128:(ft + 1) * 128], self.identf)
            o1 = osq
            c.tt("dve", f3(o1), f3(bt_), ln_g.un(2).bc([128, 4, 128]), ALU.mult)
            c.tt("dve", f3(o1), f3(o1), ln_b.un(2).bc([128, 4, 128]), ALU.add)
            c.tt("dve", o1, o1, bonus, ALU.add)
            c.tt("dve", mixT[s][:, 4:8, :], f3(o1), f3(gT), ALU.mult)

        def stage_out(i, s):
            for half in range(2):
                bkp = c.bank()
                for k in range(8):
                    c.mm(bkp, mixT[s][:, k, :], w_out[:, k, half * 512:(half + 1) * 512], start=(k == 0), stop=(k == 7))
                hs = hb[s][:, half * 512:(half + 1) * 512]
                c.tt("dve", hs, bkp, hs, ALU.add)
            if isinstance(i, int):
                c.dma("act", dst(i), hb[s], "ev_o%d" % s)
            elif s == 1:
                c.dma("act", dst.pair(1), hbp, "ev_op")

        def tile(i, s, first=False):
            stage_pre(i, s, first)
            stage_ret(i, s)
            stage_rwkv(i, s)
            stage_out(i, s)

        tile(0, 0, first=True)
        tile(1, 1)
        if NT > 2:
            c.S.new_segment(loop=(0, NT // 2 - 1))
            tile("L", 0)
            tile("L", 1)
            c.S.new_segment()


def default_plan():
    plan = []
    for layer in range(4):
        plan.append(("E" if layer % 2 == 0 else "O", layer, False))
        plan.append(("B", layer, layer == 3))
    return plan


_CACHE = {}


def run_prog(T, plan, per_core_inputs):
    key = (T, tuple(plan))
    if key not in _CACHE:
        p = Prog(T, plan)
        _CACHE[key] = (p.build(), set(p.dins))
    nc, used = _CACHE[key]
    ctab = make_ctab()
    rope = make_rope(T)
    maps = []
    for m in per_core_inputs:
        m = dict(m)
        m["ctab"] = ctab
        m["rope"] = rope
        maps.append({k: v for k, v in m.items() if k in used})
    res = run_bass_kernel_spmd(nc, maps, core_ids=list(range(len(maps))))
    return [r["y"] for r in res.results]


def kernel(**inputs):
    x = np.asarray(inputs["x"], np.float32)
    B, T, _ = x.shape
    shared = {}
    for k, v in inputs.items():
        if k == "x":
            continue
        a = np.ascontiguousarray(np.asarray(v, np.float32))
        if k in ("rwkv_r_k", "rwkv_ln_g", "rwkv_ln_b"):
            a = a.reshape(a.shape[0], -1)
        shared[k] = a
    maps = []
    for b in range(B):
        m = dict(shared)
        m["x"] = np.ascontiguousarray(x[b])
        maps.append(m)
    outs = run_prog(T, default_plan(), maps)
    return np.stack(outs, axis=0).astype(np.float32)
```
